# Optimizing a Trainium2 kernel written in Bass

```python
import jax, jax.numpy as jnp
from jax import lax
import numpy as np

D_MODEL = 1024
BATCH = 8
SEQ = 2048
DEPTH = 1
DEC_BATCH = 128
DEC_SEQ = 4
PAST_LEN = 16384
PAGE_SIZE = 128

N_META = 16
POOL_WINDOWS = (2, 4, 8, 16)
N_POOL_GROUPS = len(POOL_WINDOWS)
POOL_W = D_MODEL // 2
POOL_GC = POOL_W // N_POOL_GROUPS
POOL_BUF = max(POOL_WINDOWS) - 1
CONV_W = D_MODEL - POOL_W
CONV_HEADS = 4
CONV_K = 3
CONV_BUF = CONV_K - 1
MIX_W = POOL_W + CONV_W
IN_W = POOL_W + 3 * CONV_W
D_FF = 4 * D_MODEL
EPS = 1e-6

kernel_name = "hymba_pool_shortconv_decode_step"


def rmsnorm(x, g):
    xf = x.astype(jnp.float32)
    y = xf * lax.rsqrt(jnp.mean(xf * xf, axis=-1, keepdims=True) + EPS)
    return (y * g.astype(jnp.float32)).astype(x.dtype)


def pool_mixer(u, u_past, p0, pool_w, pool_scale):
    b, t, _ = u.shape
    ext = jnp.concatenate([u_past, u], axis=1)
    cs = jnp.cumsum(ext.astype(jnp.float32), axis=1)
    cs = jnp.pad(cs, ((0, 0), (1, 0), (0, 0)))
    hi = cs[:, POOL_BUF + 1:]
    pos = p0 + jnp.arange(t, dtype=jnp.int32)
    means = []
    for g, w in enumerate(POOL_WINDOWS):
        sl = slice(g * POOL_GC, (g + 1) * POOL_GC)
        lo = cs[:, POOL_BUF + 1 - w: POOL_BUF + 1 - w + t, sl]
        cnt = jnp.minimum(w, pos + 1).astype(jnp.float32)
        means.append((hi[..., sl] - lo) / cnt[None, :, None])
    mean = jnp.concatenate(means, axis=-1)
    d = (mean - u.astype(jnp.float32)).astype(u.dtype).reshape(b, t, N_POOL_GROUPS, POOL_GC)
    out = jnp.einsum('btgc,gcd->btgd', d, pool_w).reshape(b, t, POOL_W)
    return out * pool_scale, ext[:, -POOL_BUF:]


def conv_mixer(bg, cg, h, z_past, conv_w):
    t = h.shape[1]
    z = cg * h
    ext = jnp.concatenate([z_past, z], axis=1)
    y = ext[:, 0:t] * conv_w[0] + ext[:, 1:1 + t] * conv_w[1] + ext[:, 2:2 + t] * conv_w[2]
    return bg * y, ext[:, -CONV_BUF:]


def layer(x, s_pool, s_conv, p0, norm1_g, w_in, pool_w, pool_scale, conv_w, w_out,
          norm2_g, w1, w2):
    hn = rmsnorm(x, norm1_g)
    proj = jnp.einsum('btd,de->bte', hn, w_in)
    u = proj[..., :POOL_W]
    bg = proj[..., POOL_W:POOL_W + CONV_W]
    cg = proj[..., POOL_W + CONV_W:POOL_W + 2 * CONV_W]
    hc = proj[..., POOL_W + 2 * CONV_W:]
    ya, new_pool = pool_mixer(u, s_pool, p0, pool_w, pool_scale)
    yb, new_conv = conv_mixer(bg, cg, hc, s_conv, conv_w)
    mix = jnp.concatenate([ya, yb], axis=-1)
    x = x + jnp.einsum('bte,ed->btd', mix, w_out)
    hn = rmsnorm(x, norm2_g)
    a = jax.nn.relu(jnp.einsum('btd,df->btf', hn, w1))
    x = x + jnp.einsum('btf,fd->btd', a * a, w2)
    return x, new_pool, new_conv


def setup_inputs(seed: int = 0) -> dict:
    key = jax.random.key(seed)
    ks = jax.random.split(key, 16)
    f32 = jnp.float32
    n = lambda k, s, sc: jax.random.normal(k, s, f32) * sc
    return {
        "x_prompt": n(ks[0], (BATCH, SEQ, D_MODEL), 1.0),
        "x_sample": n(ks[1], (DEC_BATCH, DEC_SEQ, D_MODEL), 1.0),
        "state_pool": n(ks[2], (DEPTH, DEC_BATCH, POOL_BUF, POOL_W), 1.0),
        "state_conv": n(ks[3], (DEPTH, DEC_BATCH, CONV_BUF, CONV_W), 1.0),
        "meta_tokens": n(ks[4], (N_META, D_MODEL), 1.0),
        "norm1_g": 1.0 + n(ks[5], (DEPTH, D_MODEL), 0.05),
        "w_in": n(ks[6], (DEPTH, D_MODEL, IN_W), D_MODEL ** -0.5),
        "pool_w": n(ks[7], (DEPTH, N_POOL_GROUPS, POOL_GC, POOL_GC), POOL_GC ** -0.5),
        "pool_scale": 1.0 + n(ks[8], (DEPTH, POOL_W), 0.1),
        "conv_w": n(ks[9], (DEPTH, CONV_K, CONV_W), CONV_K ** -0.5),
        "w_out": n(ks[10], (DEPTH, MIX_W, D_MODEL), MIX_W ** -0.5),
        "norm2_g": 1.0 + n(ks[11], (DEPTH, D_MODEL), 0.05),
        "w1": n(ks[12], (DEPTH, D_MODEL, D_FF), D_MODEL ** -0.5),
        "w2": n(ks[13], (DEPTH, D_FF, D_MODEL), D_FF ** -0.5),
        "final_g": 1.0 + n(ks[14], (D_MODEL,), 0.05),
    }


def reference(x_prompt, x_sample, state_pool, state_conv, meta_tokens, norm1_g, w_in,
              pool_w, pool_scale, conv_w, w_out, norm2_g, w1, w2, final_g):
    b = x_prompt.shape[0]
    meta = jnp.broadcast_to(meta_tokens.astype(x_prompt.dtype)[None], (b, N_META, D_MODEL))
    xp = jnp.concatenate([meta, x_prompt], axis=1)
    xs = x_sample
    zp_pool = jnp.zeros((b, POOL_BUF, POOL_W), x_prompt.dtype)
    zp_conv = jnp.zeros((b, CONV_BUF, CONV_W), x_prompt.dtype)
    pp, cp, ps, cs = [], [], [], []
    for l in range(DEPTH):
        params = (norm1_g[l], w_in[l], pool_w[l], pool_scale[l], conv_w[l], w_out[l],
                  norm2_g[l], w1[l], w2[l])
        xp, np_pool, np_conv = layer(xp, zp_pool, zp_conv, 0, *params)
        xs, ns_pool, ns_conv = layer(xs, state_pool[l], state_conv[l], PAST_LEN, *params)
        pp.append(np_pool); cp.append(np_conv); ps.append(ns_pool); cs.append(ns_conv)
    y_prompt = rmsnorm(xp, final_g)[:, N_META:]
    y_sample = rmsnorm(xs, final_g)
    new_pool_prompt = jnp.stack(pp, axis=0)
    new_conv_prompt = jnp.stack(cp, axis=0)
    new_pool_sample = jnp.stack(ps, axis=0)
    new_conv_sample = jnp.stack(cs, axis=0)
    return (y_prompt, y_sample, new_pool_prompt, new_conv_prompt, new_pool_sample, new_conv_sample)
```

```python
import contextlib
import numpy as np
import concourse.bass as bass
import concourse.mybir as mybir
from concourse.bass_utils import run_bass_kernel_spmd

F32 = mybir.dt.float32
BF16 = mybir.dt.bfloat16
U8 = mybir.dt.uint8
ALU = mybir.AluOpType
AF = mybir.ActivationFunctionType

N_CORES = 8
D = 1024
SEQ = 2048
NMETA = 16
NSEQ_S = 16
TS = 4
EPS = 1e-6
POOL_WIN = (2, 4, 8, 16)

COMPUTE = ("tensor", "vector", "scalar", "gpsimd")
QUEUES = ("sync", "tensor", "vector", "scalar", "gpsimd")


class _Op:
    __slots__ = ("eng", "fn", "deps", "signal", "count", "dma_sem", "dma_cnt", "idx")


class Prog:
    def __init__(self):
        self.streams = {q: [] for q in QUEUES}
        self.last_w = {}
        self.readers = {}
        self.bykey0 = {}
        self.dma_counts = {}

    def keys_of(self, *bufnames):
        out = []
        for b in bufnames:
            out.extend(self.bykey0.get(b, ()))
        return out

    def _collect(self, eng, reads, writes, is_dma):
        deps = set()
        for r in reads:
            ev = self.last_w.get(r)
            if ev is not None and (ev[0] == "D" or is_dma or ev[1] != eng or eng != "tensor"):
                deps.add(ev)
        for w in writes:
            ev = self.last_w.get(w)
            if ev is not None and (ev[0] == "D" or is_dma or ev[1] != eng or eng != "tensor"):
                deps.add(ev)
            for ev in self.readers.get(w, ()):
                if ev[0] == "D" or is_dma or ev[1] != eng or eng != "tensor":
                    deps.add(ev)
        return deps

    def _register(self, ev, reads, writes):
        for k in list(reads) + list(writes):
            s = self.bykey0.setdefault(k[0] if isinstance(k, tuple) else k, set())
            s.add(k)
        for r in reads:
            self.readers.setdefault(r, []).append(ev)
        for w in writes:
            self.last_w[w] = ev
            self.readers[w] = []

    def add(self, eng, fn, reads=(), writes=()):
        op = _Op()
        op.eng = eng; op.fn = fn; op.signal = False; op.count = None
        op.dma_sem = None; op.dma_cnt = None
        op.deps = self._collect(eng, reads, writes, False)
        op.idx = len(self.streams[eng])
        self.streams[eng].append(op)
        self._register(("E", eng, op.idx), reads, writes)
        return op

    def dma(self, queue, fn, reads=(), writes=(), sem="dma"):
        op = _Op()
        op.eng = queue; op.fn = fn; op.signal = False; op.count = None
        op.deps = self._collect(queue, reads, writes, True)
        cnt = self.dma_counts.get(sem, 0) + 16
        self.dma_counts[sem] = cnt
        op.dma_sem = sem; op.dma_cnt = cnt
        op.idx = len(self.streams[queue])
        self.streams[queue].append(op)
        self._register(("D", sem, cnt), reads, writes)
        return op

    def emit(self, nc, final_waits=()):
        streams = self.streams
        for q in QUEUES:
            for op in streams[q]:
                for ev in op.deps:
                    if ev[0] == "E":
                        streams[ev[1]][ev[2]].signal = True
        for q in QUEUES:
            c = 0
            for op in streams[q]:
                if op.dma_sem is None and op.signal:
                    c += 1
                    op.count = c
        with contextlib.ExitStack() as st:
            esem = {e: st.enter_context(nc.semaphore("s_" + e)) for e in COMPUTE}
            dsem = {n: st.enter_context(nc.semaphore("d_" + n)) for n in self.dma_counts}
            block = st.enter_context(nc.Block())

            def run(q, eng):
                seen = {}
                for op in streams[q]:
                    waits = {}
                    for ev in op.deps:
                        if ev[0] == "E":
                            s = esem[ev[1]]; v = streams[ev[1]][ev[2]].count; key = "E" + ev[1]
                        else:
                            s = dsem[ev[1]]; v = ev[2]; key = "D" + ev[1]
                        if seen.get(key, 0) >= v:
                            continue
                        if key not in waits or waits[key][1] < v:
                            waits[key] = (s, v)
                    for key, (s, v) in waits.items():
                        eng.wait_ge(s, v)
                        seen[key] = v
                    ins = op.fn(eng)
                    if op.dma_sem is not None:
                        ins.then_inc(dsem[op.dma_sem], 16)
                    elif op.signal:
                        ins.then_inc(esem[q], 1)
                if q == "sync":
                    for n in final_waits:
                        eng.wait_ge(dsem[n], self.dma_counts[n])

            @block.sync
            def _(e):
                run("sync", e)

            @block.tensor
            def _(e):
                run("tensor", e)

            @block.vector
            def _(e):
                run("vector", e)

            @block.scalar
            def _(e):
                run("scalar", e)

            @block.gpsimd
            def _(e):
                run("gpsimd", e)


def build_nc():
    nc = bass.Bass("TRN2", target_bir_lowering=False)

    def din(name, shape):
        return nc.dram_tensor(name, list(shape), F32, kind="ExternalInput").ap()

    def dout(name, shape):
        return nc.dram_tensor(name, list(shape), F32, kind="ExternalOutput").ap()

    xp = din("xp", [SEQ, D]); xs = din("xs", [NSEQ_S, TS, D])
    sp = din("sp", [NSEQ_S, 15, 512]); sc = din("sc", [NSEQ_S, 2, 512])
    meta = din("meta", [NMETA, D]); g1 = din("g1", [D]); w_in = din("w_in", [D, 2048])
    pool_w = din("pool_w", [4, 128, 128]); pool_scale = din("pool_scale", [512]); conv_w = din("conv_w", [3, 512])
    w_out = din("w_out", [D, D]); g2 = din("g2", [D]); w1 = din("w1", [D, 4096]); w2 = din("w2", [4096, D])
    fg = din("fg", [D])
    yp = dout("yp", [SEQ, D]); ys = dout("ys", [NSEQ_S, TS, D])
    npp = dout("npp", [15, 512]); ncp = dout("ncp", [2, 512])
    nps = dout("nps", [NSEQ_S, 15, 512]); ncs = dout("ncs", [NSEQ_S, 2, 512])

    TOTAL = 212736
    big = nc.alloc_sbuf_tensor("big", [128, TOTAL], U8)
    cur = [0]

    def carve(nbytes, dt, pattern=None, at=None, **kw):
        if at is None:
            off = cur[0]
            cur[0] += (nbytes + 63) // 64 * 64
        else:
            off = at
        assert off + nbytes <= TOTAL, (off, nbytes)
        v = big[:, off:off + nbytes].bitcast(dt)
        if pattern:
            v = v.rearrange(pattern, **kw)
        return v

    NT = 17
    NCOL = SEQ + 64
    X1 = carve(NT * D * 4, F32, "p (t d) -> p t d", t=NT)
    H2T = carve(8 * NCOL * 2, BF16, "p (k n) -> p k n", k=8)
    identb = carve(128 * 2, BF16)
    identf = carve(128 * 4, F32)
    CONST = carve(32 * 4, F32)
    g1T = CONST[:, 0:8]; g2T = CONST[:, 8:16]; psc = CONST[:, 16:20]
    cw = CONST[:, 20:32].rearrange("p (j c) -> p j c", j=3)
    PSC = carve(4 * 4, F32)
    NSLOT = 16
    ss = carve(NSLOT * 4, F32); sd = carve(NSLOT * 4, F32); rs = carve(NSLOT * 4, F32)
    R0 = cur[0]
    WIN = carve(8 * 2048 * 2, BF16, "p (k e) -> p k e", k=8)
    WOUT = carve(8 * 1024 * 2, BF16, "p (k e) -> p k e", k=8)
    R1 = cur[0]
    PW = carve(4 * 128 * 2, BF16, "p (g d) -> p g d", g=4)
    HN = [carve(D * 2, BF16) for _ in range(2)]
    JA = carve(D * 2, BF16)
    HNT = carve(8 * 512 * 2, BF16, "p (k n) -> p k n", k=8)
    UL = 528
    UB = carve(4 * UL * 4, F32, "p (c n) -> p c n", c=4)
    CGB = [carve(512 * 4, F32) for _ in range(2)]
    ZL = 516
    ZB = carve(4 * ZL * 4, F32, "p (c n) -> p c n", c=4)
    PT = [carve(UL * 4, F32) for _ in range(2)]
    CV = [carve(512 * 4, F32) for _ in range(2)]
    DT = carve(4 * 512 * 2, BF16, "p (c n) -> p c n", c=4)
    MIX = carve(8 * 512 * 2, BF16, "p (k n) -> p k n", k=8)
    endA = cur[0]
    W1B = [carve(8 * 1024 * 2, BF16, "p (k f) -> p k f", k=8, at=R0 + i * 16384) for i in range(2)]
    W2B = [carve(8 * 1024 * 2, BF16, "p (c d) -> p c d", c=8, at=R0 + 32768 + i * 16384) for i in range(2)]
    o = R0 + 65536
    A2T = [carve(8 * 512 * 2, BF16, "p (c n) -> p c n", c=8, at=o + i * 8192) for i in range(2)]
    o += 16384
    RB = [carve(512 * 4, F32, at=o + i * 2048) for i in range(2)]
    o += 4096
    FG = carve(D * 4, F32, at=o); o += 4096
    JUNK = carve(D * 2, BF16, at=o); o += 2048
    assert o <= TOTAL and endA <= TOTAL, (o, endA)
    W2B1_ALIAS = ("PW", "HN", "JA", "HNT", "UB")
    MIXERS = ("UB", "CGB", "ZB", "PT", "CV")
    assert (endA - R0) >= 0 and (R0 + 92160) <= (R0 + 93632)

    def h2t_f32(k, lo, n):
        return H2T[:, k, 512 + lo // 2:512 + lo // 2 + 2 * n].bitcast(F32)

    def h2t_bf(k, lo, n):
        return H2T[:, k, 512 + lo // 2:512 + lo // 2 + n]

    UBS0 = h2t_f32(0, 0, 2 * 304).rearrange("p (c n) -> p c n", c=2)
    UBS1 = h2t_f32(1, 0, 2 * 304).rearrange("p (c n) -> p c n", c=2)
    ZBS = h2t_f32(2, 0, 4 * 96).rearrange("p (c n) -> p c n", c=4)
    CVS = [h2t_f32(2, 1536 + i * 256, 64) for i in range(4)]
    PTS = [h2t_f32(3, i * 1216, 304) for i in range(2)]
    CGBS = [h2t_f32(4, i * 256, 64) for i in range(2)]
    DTS = h2t_bf(4, 512, 256).rearrange("p (c n) -> p c n", c=4)
    MIXS = h2t_bf(4, 1024, 512).rearrange("p (k n) -> p k n", k=8)
    HNTS = h2t_bf(4, 2048, 512).rearrange("p (k n) -> p k n", k=8)
    SL = [h2t_f32(5 + i, 0, 512) for i in range(3)]

    class UBsplit:
        def __getitem__(self, idx):
            p, c, n = idx
            if isinstance(c, slice):
                raise TypeError
            return (UBS0 if c < 2 else UBS1)[p, c % 2, n]

    BM = dict(UB=UB, ZB=ZB, CGB=CGB, CV=CV, PT=PT, DT=DT, MIX=MIX, HNT=HNT, n="")
    BS = dict(UB=UBsplit(), ZB=ZBS, CGB=CGBS, CV=CVS, PT=PTS, DT=DTS, MIX=MIXS, HNT=HNTS, n="S")
    S_PRIVATE = ("UBS", "ZBS", "CGBS", "CVS", "PTS", "DTS", "MIXS", "HNTS", "SL")

    psall = nc.alloc_psum_tensor("psall", [128, 4096], F32).ap()
    NFM = 4
    psFM = [psall[:, i * 512:(i + 1) * 512] for i in range(NFM)]
    psTM = [psall[:, 2048:3072], psall[:, 3072:4096]]
    psT = [psall[:, 3072:3584].bitcast(BF16), psall[:, 3584:4096].bitcast(BF16),
           psall[:, 2048:2560].bitcast(BF16), psall[:, 2560:3072].bitcast(BF16)]
    TMK = [[("ps", 4), ("ps", 5)], [("ps", 6), ("ps", 7)]]
    TK = [("ps", 6), ("ps", 7), ("ps", 4), ("ps", 5)]

    P = Prog()
    cnt = {"fm": 0, "tm": 0, "t": 0, "slot": 0, "hn": 0}

    def nxt(name, mod):
        v = cnt[name] % mod
        cnt[name] += 1
        return v

    groups = []
    groups.append(dict(name="meta", N=32, tiles=[(15, 32)], stride=1, H=16, Hz=2, col0=None, kind="meta"))
    groups.append(dict(name="P0", N=512, tiles=[(i, 128) for i in range(4)], stride=1, H=16, Hz=2, col0=0, kind="prompt", last=False))
    groups.append(dict(name="S", N=64, tiles=[(16, 64)], stride=16, H=240, Hz=32, col0=SEQ, kind="sample"))
    for g in range(1, 4):
        groups.append(dict(name="P%d" % g, N=512, tiles=[(4 * g + i, 128) for i in range(4)], stride=1, H=16, Hz=2,
                           col0=512 * g, kind="prompt", last=(g == 3)))

    for G in groups:
        G["B"] = BS if G["kind"] == "sample" else BM

    spf = sp.rearrange("s r f -> (s r) f")
    scf = sc.rearrange("s r f -> (s r) f")

    def load_state():
        P.dma("sync", lambda e: e.dma_start(out=SL[0][0:120, :], in_=spf[0:120, :]), writes=[("SL", 0)], sem="stateA")
        P.dma("sync", lambda e: e.dma_start(out=SL[1][0:120, :], in_=spf[120:240, :]), writes=[("SL", 1)], sem="stateB")
        P.dma("sync", lambda e: e.dma_start(out=SL[2][0:32, :], in_=scf), writes=[("SL", 2)], sem="stateC")

    CST = PT[1]
    P.dma("gpsimd", lambda e: e.dma_start(out=CST[0:8, 0:128], in_=g1.rearrange("(k p) -> k p", p=128)), writes=[("CSTsub", 0)], sem="cst")
    P.dma("gpsimd", lambda e: e.dma_start(out=CST[8:16, 0:128], in_=g2.rearrange("(k p) -> k p", p=128)), writes=[("CSTsub", 1)], sem="cst")
    P.dma("gpsimd", lambda e: e.dma_start(out=CST[16:20, 0:128], in_=pool_scale.rearrange("(g p) -> g p", p=128)), writes=[("CSTsub", 2)], sem="cst")
    P.dma("gpsimd", lambda e: e.dma_start(out=CST[20:32, 0:128], in_=conv_w.rearrange("j (c p) -> (j c) p", p=128)), writes=[("CSTsub", 3)], sem="cst")
    P.dma("sync", lambda e: e.dma_start(out=X1[0:16, 15, :], in_=meta), writes=[("X1sub", 15, 0)], sem="xmeta")
    P.dma("sync", lambda e: e.dma_start(out=X1[16:32, 15, :], in_=meta), writes=[("X1sub", 15, 1)], sem="xmeta")

    def load_xp(g, gate=()):
        P.dma("sync", lambda e, g=g: e.dma_start(out=X1[:, 4 * g:4 * g + 4, :],
                                                  in_=xp[512 * g:512 * (g + 1), :].rearrange("(t p) d -> p t d", p=128)),
              reads=list(gate), writes=[("X1", 4 * g + i) for i in range(4)], sem="xP%d" % g)

    load_xp(0)

    qorder = [(0, 0), (2, 1024), (3, 1536), (1, 512)]
    for qi, e0 in qorder:
        P.dma("gpsimd", lambda e, e0=e0: e.dma_start(out=WIN[:, :, e0:e0 + 512],
                                                      in_=w_in[:, e0:e0 + 512].rearrange("(k p) e -> p k e", p=128)),
              writes=[("WIN", qi)], sem="win%d" % qi)
    P.dma("gpsimd", lambda e: e.dma_start(out=PW, in_=pool_w.rearrange("g c d -> c g d")), writes=[("PW",)], sem="pw")
    P.dma("gpsimd", lambda e: e.dma_start(out=WOUT, in_=w_out.rearrange("(k p) e -> p k e", p=128)), writes=[("WOUT",)], sem="wout")

    P.add("gpsimd", lambda e: e.memset(identf, 0.0), writes=[("identf",)])
    P.add("gpsimd", lambda e: e.affine_select(out=identf, in_=identf, compare_op=ALU.not_equal, fill=1.0, base=0,
                                               pattern=[[-1, 128]], channel_multiplier=1),
          reads=[("identf",)], writes=[("identf",)])
    P.add("vector", lambda e: e.tensor_copy(out=identb, in_=identf), reads=[("identf",)], writes=[("identb",)])
    P.add("tensor", lambda e: e.transpose(out=psFM[0][:, 0:32], in_=CST[0:32, 0:128], identity=identf[0:32, 0:32]),
          reads=[("PT", 1), ("identf",)] + [("CSTsub", i) for i in range(4)], writes=[("ps", 0)])
    P.add("vector", lambda e: e.tensor_copy(out=CONST, in_=psFM[0][:, 0:32]), reads=[("ps", 0)], writes=[("CONST",)])
    cnt["fm"] = 1
    CK = ("CONST",)
    for c in range(4):
        P.add("vector", lambda e, c=c: e.tensor_scalar(out=PSC[:, c:c + 1], in0=psc[:, c:c + 1], scalar1=1.0 / POOL_WIN[c], scalar2=None,
                                                        op0=ALU.mult), reads=[CK], writes=[("PSC",)])

    lazy_recips = []

    def flush_recips():
        while lazy_recips:
            lazy_recips.pop(0)()

    def stats(src, pn, src_keys, junk, junk_key, defer=False):
        slot = nxt("slot", NSLOT)
        P.add("scalar", lambda e: e.activation(out=junk[0:pn, :], in_=src, func=AF.Square, accum_out=ss[0:pn, slot:slot + 1]),
              reads=src_keys, writes=[junk_key, ("ss", slot)])
        P.add("scalar", lambda e: e.activation(out=sd[0:pn, slot:slot + 1], in_=ss[0:pn, slot:slot + 1], func=AF.Sqrt,
                                               bias=EPS, scale=1.0 / D),
              reads=[("ss", slot)], writes=[("sd", slot)])
        def rec():
            P.add("vector", lambda e: e.reciprocal(out=rs[0:pn, slot:slot + 1], in_=sd[0:pn, slot:slot + 1]),
                  reads=[("sd", slot)], writes=[("rs", slot)])
        if defer:
            lazy_recips.append(rec)
        else:
            rec()
        return slot

    def scale_to_hn(src, pn, src_keys, slot):
        flush_recips()
        hi = nxt("hn", 2)
        hn = HN[hi][0:pn, :]
        P.add("scalar", lambda e: e.activation(out=hn, in_=src, func=AF.Copy, scale=rs[0:pn, slot:slot + 1]),
              reads=src_keys + [("rs", slot)], writes=[("HN", hi)])
        return hn, ("HN", hi)

    def transpose_to_fm(hn, hn_key, pn, gT, dst, dst_keys):
        tb = nxt("t", 4)
        for k in range(8):
            P.add("tensor", lambda e, k=k: e.transpose(out=psT[tb][:, k * 128:k * 128 + pn], in_=hn[:, k * 128:(k + 1) * 128],
                                                       identity=identb[0:pn, 0:pn]),
                  reads=[hn_key, ("identb",)], writes=[TK[tb]])
        src = psT[tb].rearrange("p (k n) -> p k n", k=8)[:, :, 0:pn]
        P.add("vector", lambda e: e.tensor_tensor(out=dst, in0=src, in1=gT.unsqueeze(2).to_broadcast([128, 8, pn]), op=ALU.mult),
              reads=[TK[tb], CK], writes=dst_keys)

    def fm_matmul(lhs_fn, rhs_fn, nk, N, reads):
        b = nxt("fm", NFM)
        for k in range(nk):
            P.add("tensor", lambda e, k=k: e.matmul(out=psFM[b][:, 0:N], lhsT=lhs_fn(k), rhs=rhs_fn(k), start=(k == 0), stop=(k == nk - 1)),
                  reads=reads, writes=[("ps", b)])
        return b, psFM[b][:, 0:N]

    def xkeys(G, ti):
        if G["kind"] == "sample":
            return [("X1", ti)] + [("X1sub", 16, t) for t in range(TS)]
        if G["kind"] == "meta":
            return [("X1", ti), ("X1sub", 15, 0), ("X1sub", 15, 1)]
        return [("X1", ti)]

    def pre_stats1(G, tiles=None, defer=False):
        G.setdefault("slots1", {})
        for i, (ti, pn) in enumerate(G["tiles"]):
            if (tiles is not None and i not in tiles) or i in G["slots1"]:
                continue
            G["slots1"][i] = stats(X1[0:pn, ti, :], pn, xkeys(G, ti), JA, ("JA",), defer=defer)

    HM = MIX[:, 7, 0:256].rearrange("p (k n) -> p k n", k=8)

    def hnt_key(G, i):
        return ("MIX", 7) if G["kind"] == "meta" else ("HNT" + G["B"]["n"], i)

    def T1_prescale(G, n=2):
        G.setdefault("hn1", {})
        for i, (ti, pn) in enumerate(G["tiles"][:n]):
            if i not in G["hn1"]:
                G["hn1"][i] = scale_to_hn(X1[0:pn, ti, :], pn, xkeys(G, ti), G["slots1"][i])

    def T1(G):
        B = G["B"]; sfx = B["n"]
        UB = B["UB"]; ZB = B["ZB"]; CGB = B["CGB"]; CV = B["CV"]; PT = B["PT"]; DT = B["DT"]; MIX = B["MIX"]; HNT = B["HNT"]
        KUB = "UB" + sfx; KZB = "ZB" + sfx; KCGB = "CGB" + sfx; KCV = "CV" + sfx; KPT = "PT" + sfx; KDT = "DT" + sfx
        KMIX = "MIX" + sfx; KHNT = "HNT" + sfx; NCV = len(CV)
        G.setdefault("hn1", {})
        for i, (ti, pn) in enumerate(G["tiles"]):
            if i not in G["hn1"]:
                G["hn1"][i] = scale_to_hn(X1[0:pn, ti, :], pn, xkeys(G, ti), G["slots1"][i])
            hn, hk = G["hn1"][i]
            dst = HM[:, :, 0:pn] if G["kind"] == "meta" else HNT[:, :, i * 128:i * 128 + pn]
            transpose_to_fm(hn, hk, pn, g1T, dst, [hnt_key(G, i)])
            if i + 2 < len(G["tiles"]):
                T1_prescale(G, i + 3)

    class T2Item:
        def __init__(self, x1, pn, ti, slot, c0):
            self.x1 = x1; self.pn = pn; self.ti = ti; self.slot = slot; self.c0 = c0; self.hn = None

        def scale(self):
            if self.hn is None:
                self.hn = scale_to_hn(self.x1, self.pn, [("X1", self.ti)], self.slot)

        def trans(self):
            self.scale()
            hn, hk = self.hn
            extra = P.keys_of(*S_PRIVATE) if 4 <= self.ti < 16 else []
            transpose_to_fm(hn, hk, self.pn, g2T, H2T[:, :, self.c0:self.c0 + self.pn], [("H2T", self.ti)] + extra)

    def pop_pending(pending, n=1):
        for _ in range(n):
            if pending:
                it = pending.pop(0)
                it.trans()
                if len(pending) >= 2:
                    pending[1].scale()
                elif len(pending) == 1:
                    pending[0].scale()

    def P2(G, pending, nxtG, deferred=None, stages=("in", "u", "cgh", "rest")):
        B = G["B"]; sfx = B["n"]
        UB = B["UB"]; ZB = B["ZB"]; CGB = B["CGB"]; CV = B["CV"]; PT = B["PT"]; DT = B["DT"]; MIX = B["MIX"]; HNT = B["HNT"]
        KUB = "UB" + sfx; KZB = "ZB" + sfx; KCGB = "CGB" + sfx; KCV = "CV" + sfx; KPT = "PT" + sfx; KDT = "DT" + sfx
        KMIX = "MIX" + sfx; KHNT = "HNT" + sfx; NCV = len(CV)
        N = G["N"]; s = G["stride"]; H = G["H"]; Hz = G["Hz"]; kind = G["kind"]
        zo = G.get("zo", 0)
        L = H + N
        mixing = kind != "meta"
        if kind == "sample" and "in" in stages:
            for c in range(4):
                for half in range(2):
                    b = nxt("fm", NFM)
                    P.add("tensor", lambda e, c=c, half=half, b=b: e.transpose(
                        out=psFM[b][:, 0:120], in_=SL[half][0:120, c * 128:(c + 1) * 128], identity=identf[0:120, 0:120]),
                        reads=[("SL", half), ("identf",)], writes=[("ps", b)])
                    dst = UB[:, c, 0:240].rearrange("p (r s) -> p s r", s=16)[:, half * 8:(half + 1) * 8, :]
                    P.add("scalar", lambda e, dst=dst, b=b: e.copy(out=dst, in_=psFM[b][:, 0:120].rearrange("p (s r) -> p s r", r=15)),
                          reads=[("ps", b)], writes=[(KUB, c, "halo")])
                b = nxt("fm", NFM)
                P.add("tensor", lambda e, c=c, b=b: e.transpose(out=psFM[b][:, 0:32], in_=SL[2][0:32, c * 128:(c + 1) * 128],
                                                                identity=identf[0:32, 0:32]),
                      reads=[("SL", 2), ("identf",)], writes=[("ps", b)])
                dstz = ZB[:, c, 0:32].rearrange("p (r s) -> p s r", s=16)
                P.add("scalar", lambda e, dstz=dstz, b=b: e.copy(out=dstz, in_=psFM[b][:, 0:32].rearrange("p (s r) -> p s r", r=2)),
                      reads=[("ps", b)], writes=[(KZB, c, "halo")])
        hnt_keys = [hnt_key(G, i) for i in range(len(G["tiles"]))]
        RH = HM if kind == "meta" else HNT

        def win_mm(e0, qi):
            return fm_matmul(lambda k: WIN[:, k, e0:e0 + 128], lambda k: RH[:, k, 0:N], 8, N, [("WIN", qi)] + hnt_keys)

        dstate = {}

        def pool_adds(c):
            w = POOL_WIN[c]
            u = UB[:, c, :]
            lo = H - (w - 2) * s
            cur_src, cur_keys = u, [(KUB, c, "halo"), (KUB, c, "new")]
            step = s
            bi = 0
            for lev in range({2: 1, 4: 2, 8: 3, 16: 4}[w]):
                dstb = PT[bi]
                P.add("gpsimd", lambda e, dstb=dstb, cur_src=cur_src, a0=lo, step=step: e.tensor_tensor(
                    out=dstb[:, a0:L], in0=cur_src[:, a0:L], in1=cur_src[:, a0 - step:L - step], op=ALU.add),
                    reads=cur_keys, writes=[(KPT, bi)])
                cur_src, cur_keys = dstb, [(KPT, bi)]
                lo += 2 * step
                step *= 2
                bi ^= 1
            tmp = PT[bi]
            P.add("gpsimd", lambda e: e.tensor_scalar(out=tmp[:, H:L], in0=u[:, H:L], scalar1=-float(w), scalar2=0.0, op0=ALU.mult, op1=ALU.add),
                  reads=[(KUB, c, "new")], writes=[(KPT, bi)])
            P.add("gpsimd", lambda e: e.tensor_tensor(out=DT[:, c, 0:N], in0=cur_src[:, H:L], in1=tmp[:, H:L], op=ALU.add),
                  reads=cur_keys + [(KPT, bi)], writes=[(KDT, c)])

        def halo_u():
            P.add("gpsimd", lambda e: e.tensor_copy(out=UB[:, :, 0:16], in_=UB[:, :, L - 16:L]),
                  reads=[(KUB, c, "new") for c in range(4)], writes=[(KUB, c, "halo") for c in range(4)])

        def halo_z():
            P.add("vector" if kind == "meta" else "gpsimd", lambda e: e.tensor_copy(out=ZB[:, :, 0:2], in_=ZB[:, :, Hz + N - 2:Hz + N]),
                  reads=[(KZB, c, "new") for c in range(4)], writes=[(KZB, c, "halo") for c in range(4)])

        if "u" in stages:
            for c in range(4):
                b, ps = win_mm(c * 128, 0)
                P.add("scalar", lambda e, c=c, ps=ps: e.copy(out=UB[:, c, H:L], in_=ps), reads=[("ps", b)], writes=[(KUB, c, "new")])
                if mixing:
                    pool_adds(c)
            pop_pending(pending)
            if kind == "meta":
                halo_u()

        def conv_chunk(c):
            z = ZB[:, c, :]
            zk = [(KZB, c, "halo"), (KZB, c, "new")]
            ci = c % NCV
            cv = CV[ci][:, 0:N]
            P.add("scalar", lambda e: e.activation(out=cv, in_=z[:, zo:zo + N], func=AF.Copy, scale=cw[:, 0, c:c + 1]),
                  reads=zk + [CK], writes=[(KCV, ci)])
            P.add("vector", lambda e: e.scalar_tensor_tensor(out=cv, in0=z[:, zo + s:zo + s + N], scalar=cw[:, 1, c:c + 1], in1=cv,
                                                             op0=ALU.mult, op1=ALU.add),
                  reads=zk + [CK, (KCV, ci)], writes=[(KCV, ci)])
            P.add("vector", lambda e: e.scalar_tensor_tensor(out=cv, in0=z[:, zo + 2 * s:zo + 2 * s + N], scalar=cw[:, 2, c:c + 1], in1=cv,
                                                             op0=ALU.mult, op1=ALU.add),
                  reads=zk + [CK, (KCV, ci)], writes=[(KCV, ci)])

        def cg_h(c):
            b, ps = win_mm(1024 + c * 128, 2)
            ci = c % 2
            P.add("scalar", lambda e, ps=ps: e.copy(out=CGB[ci][:, 0:N], in_=ps), reads=[("ps", b)], writes=[(KCGB, ci)])
            b2, ps2 = win_mm(1536 + c * 128, 3)
            P.add("vector", lambda e, ps2=ps2: e.tensor_tensor(out=ZB[:, c, Hz:Hz + N], in0=ps2, in1=CGB[ci][:, 0:N], op=ALU.mult),
                  reads=[("ps", b2), (KCGB, ci)], writes=[(KZB, c, "new")])

        def bg_chunk(c):
            ci = c % NCV
            cv = CV[ci][:, 0:N]
            b, ps = win_mm(512 + c * 128, 1)
            P.add("vector", lambda e, ps=ps: e.tensor_tensor(out=MIX[:, 4 + c, 0:N], in0=ps, in1=cv, op=ALU.mult),
                  reads=[("ps", b), (KCV, ci)], writes=[(KMIX, 4 + c)])

        if "cgh" in stages:
            nst = len(nxtG["tiles"]) if nxtG is not None else 0
            cg_h(0)
            if nst > 0:
                pre_stats1(nxtG, tiles=(0,), defer=True)
            cg_h(1)
            if nst > 1:
                pre_stats1(nxtG, tiles=(1,), defer=True)
            pop_pending(pending)
            early_conv = mixing and kind == "prompt" and "rest" in stages
            if early_conv:
                conv_chunk(0); conv_chunk(1)
            cg_h(2)
            if nst > 2:
                pre_stats1(nxtG, tiles=(2,), defer=True)
            cg_h(3)
            if nst > 3:
                pre_stats1(nxtG, tiles=(3,), defer=True)
            if deferred is not None:
                deferred()
            if "rest" in stages:
                pop_pending(pending)
            if kind == "meta":
                halo_z()
        else:
            early_conv = False
        if "rest" not in stages and "rest_a" not in stages and "rest_b" not in stages:
            return
        do_a = "rest" in stages or "rest_a" in stages
        do_b = "rest" in stages or "rest_b" in stages
        if mixing and do_a:
            if not early_conv:
                conv_chunk(0); conv_chunk(1)
            bg_chunk(0); bg_chunk(1)
            pop_pending(pending)
            if nxtG is not None and nxtG["kind"] == "prompt":
                T1_prescale(nxtG, 2 if not pending else 1)
            conv_chunk(2); conv_chunk(3)
            bg_chunk(2); bg_chunk(3)
        if do_a:
            pop_pending(pending, 8)
            if nxtG is not None and nxtG["kind"] == "prompt":
                T1_prescale(nxtG, 2)
        if not do_b:
            return
        if kind == "sample" or (kind == "prompt" and G.get("last")):
            if kind == "sample":
                ucols = (H, L); zcols = (Hz, Hz + N); nr = 64
            else:
                ucols = (L - 16, L); zcols = (Hz + N - 2, Hz + N); nr = 16
            nz = zcols[1] - zcols[0]
            stg = SL if kind == "sample" else CGB
            kstg = "SL" if kind == "sample" else KCGB
            b = nxt("fm", NFM)
            for c in range(4):
                P.add("tensor", lambda e, c=c, b=b: e.transpose(out=psFM[b][0:nr, c * 128:(c + 1) * 128], in_=UB[:, c, ucols[0]:ucols[1]],
                                                                identity=identf),
                      reads=[(KUB, c, "new"), ("identf",)], writes=[("ps", b)])
            P.add("scalar", lambda e, b=b: e.copy(out=stg[0][0:nr, :], in_=psFM[b][0:nr, :]), reads=[("ps", b)], writes=[(kstg, 0)])
            b2 = nxt("fm", NFM)
            for c in range(4):
                P.add("tensor", lambda e, c=c, b2=b2: e.transpose(out=psFM[b2][0:nz, c * 128:(c + 1) * 128], in_=ZB[:, c, zcols[0]:zcols[1]],
                                                                  identity=identf),
                      reads=[(KZB, c, "new"), ("identf",)], writes=[("ps", b2)])
            P.add("scalar", lambda e, b2=b2: e.copy(out=stg[1][0:nz, :], in_=psFM[b2][0:nz, :]), reads=[("ps", b2)], writes=[(kstg, 1)])
            if kind == "sample":
                for t in range(TS):
                    P.dma("sync", lambda e, t=t: e.dma_start(out=nps[:, 11 + t, :], in_=stg[0][t * 16:(t + 1) * 16, :]),
                          reads=[(kstg, 0)], sem="o_nps")
                for t in (2, 3):
                    P.dma("sync", lambda e, t=t: e.dma_start(out=ncs[:, t - 2, :], in_=stg[1][t * 16:(t + 1) * 16, :]),
                          reads=[(kstg, 1)], sem="o_ncs")
            else:
                P.dma("sync", lambda e: e.dma_start(out=npp, in_=stg[0][1:16, :]), reads=[(kstg, 0)], sem="o_npp")
                P.dma("sync", lambda e: e.dma_start(out=ncp, in_=stg[1][0:2, :]), reads=[(kstg, 1)], sem="o_ncp")
        if kind == "prompt" and not G.get("last"):
            halo_u()
            halo_z()

    def P3(G):
        B = G["B"]; sfx = B["n"]
        UB = B["UB"]; ZB = B["ZB"]; CGB = B["CGB"]; CV = B["CV"]; PT = B["PT"]; DT = B["DT"]; MIX = B["MIX"]; HNT = B["HNT"]
        KUB = "UB" + sfx; KZB = "ZB" + sfx; KCGB = "CGB" + sfx; KCV = "CV" + sfx; KPT = "PT" + sfx; KDT = "DT" + sfx
        KMIX = "MIX" + sfx; KHNT = "HNT" + sfx; NCV = len(CV)
        N = G["N"]
        for c in range(4):
            b, ps = fm_matmul(lambda k, c=c: PW[:, c, :], lambda k, c=c: DT[:, c, 0:N], 1, N, [("PW",), (KDT, c)])
            P.add("scalar", lambda e, c=c, ps=ps: e.activation(out=MIX[:, c, 0:N], in_=ps, func=AF.Copy, scale=PSC[:, c:c + 1]),
                  reads=[("ps", b), ("PSC",)], writes=[(KMIX, c)])

    def P4(G, pending):
        B = G["B"]; sfx = B["n"]
        UB = B["UB"]; ZB = B["ZB"]; CGB = B["CGB"]; CV = B["CV"]; PT = B["PT"]; DT = B["DT"]; MIX = B["MIX"]; HNT = B["HNT"]
        KUB = "UB" + sfx; KZB = "ZB" + sfx; KCGB = "CGB" + sfx; KCV = "CV" + sfx; KPT = "PT" + sfx; KDT = "DT" + sfx
        KMIX = "MIX" + sfx; KHNT = "HNT" + sfx; NCV = len(CV)
        for i, (ti, pn) in enumerate(G["tiles"]):
            tb = nxt("tm", 2)
            korder = (4, 5, 6, 7, 0, 1, 2, 3)
            for half in range(2):
                for kk, k in enumerate(korder):
                    P.add("tensor", lambda e, k=k, kk=kk, half=half, i=i, pn=pn, tb=tb: e.matmul(
                        out=psTM[tb][0:pn, half * 512:(half + 1) * 512], lhsT=MIX[:, k, i * 128:i * 128 + pn],
                        rhs=WOUT[:, k, half * 512:(half + 1) * 512], start=(kk == 0), stop=(kk == 7)),
                        reads=[(KMIX, k), ("WOUT",)], writes=TMK[tb])
            x1 = X1[0:pn, ti, :]
            P.add("vector", lambda e, x1=x1, pn=pn, tb=tb: e.tensor_tensor(out=x1, in0=psTM[tb][0:pn, :], in1=x1, op=ALU.add),
                  reads=TMK[tb] + [("X1", ti)], writes=[("X1", ti)])
            flush_recips()
            for it in pending[:2]:
                it.scale()
            slot = stats(x1, pn, [("X1", ti)], JA, ("JA",), defer=True)
            c0 = G["col0"] + i * 128

            pending.append(T2Item(x1, pn, ti, slot, c0))
        flush_recips()
        for it in pending[:2]:
            it.scale()

    pending = []
    deferred = [None]
    Gm, G0, GS, G1 = groups[0], groups[1], groups[2], groups[3]
    load_state()
    for t in range(TS):
        P.dma("sync", lambda e, t=t: e.dma_start(out=X1[t * 16:(t + 1) * 16, 16, :], in_=xs[:, t, :]),
              writes=[("X1sub", 16, t)], sem="xS")
    pre_stats1(Gm)
    pre_stats1(G0)
    T1(Gm)
    T1(G0)
    pre_stats1(GS)
    T1_prescale(GS, 1)
    P2(Gm, [], None, stages=("u",))
    P2(G0, pending, G1, stages=("u",))
    T1(GS)
    P2(GS, [], None, stages=("in", "u"))
    load_xp(1, gate=[("HNT", 3)])
    P2(Gm, [], None, stages=("cgh",))
    P2(G0, pending, G1, stages=("cgh",))
    P2(GS, [], None, stages=("cgh",))
    P2(GS, [], None, stages=("rest_a",))
    P2(G0, pending, G1, stages=("rest",))
    deferred[0] = (lambda: P4(GS, pending))
    load_xp(2, gate=[("MIX", 7)])
    P.dma("sync", lambda e: e.dma_start(out=nps[:, 0:11, :], in_=sp[:, 4:15, :]), sem="out")
    T1(G1)
    P3(G0)
    P3(GS)
    P2(GS, [], None, stages=("rest_b",))
    P4(G0, pending)
    for gi in range(3, len(groups)):
        G = groups[gi]
        nxtG = groups[gi + 1] if gi + 1 < len(groups) else None
        dfr = None
        if deferred[0] is not None:
            dfr, deferred[0] = deferred[0], None
        P2(G, pending, nxtG, dfr)
        if gi == 3:
            load_xp(3, gate=[("MIX", 7)])
        if nxtG is not None:
            T1(nxtG)
        P3(G)
        P4(G, pending)

    bgroups = [g for g in groups if g["kind"] == "sample"] + [g for g in groups if g["kind"] == "prompt"]
    NBLK = 4

    def load_block(b):
        i = b % 2
        extra1 = P.keys_of("WIN") if b < 2 else []
        if b == 0:
            extra2 = P.keys_of("WOUT")
        elif b == 1:
            extra2 = P.keys_of(*W2B1_ALIAS)
        else:
            extra2 = []
        P.dma("gpsimd", lambda e: e.dma_start(out=W1B[i], in_=w1[:, b * 1024:(b + 1) * 1024].rearrange("(k p) f -> p k f", p=128)),
              writes=[("W1B", i)] + extra1, sem="w1b%d" % i)
        P.dma("gpsimd", lambda e: e.dma_start(out=W2B[i], in_=w2[b * 1024:(b + 1) * 1024, :].rearrange("(c p) d -> p c d", p=128)),
              writes=[("W2B", i)] + extra2, sem="w2b%d" % i)

    order0 = [bgroups[1], bgroups[0]] + bgroups[2:]
    steps = [(b, G) for b in range(NBLK) for G in (order0 if b == 0 else bgroups)]
    first_b_write = [True]

    def w1_step(si, pending=None):
        b, G = steps[si]
        N = G["N"]; col0 = G["col0"]; ai = si % 2
        tkeys = [("H2T", ti) for ti, _ in G["tiles"]]
        for fc in range(8):
            bk, ps = fm_matmul(lambda k, fc=fc: W1B[b % 2][:, k, fc * 128:(fc + 1) * 128], lambda k: H2T[:, k, col0:col0 + N], 8, N,
                               [("W1B", b % 2)] + tkeys)
            ri = fc % 2
            extra = P.keys_of(*MIXERS) if first_b_write[0] else []
            first_b_write[0] = False
            P.add("scalar", lambda e, ri=ri, ps=ps: e.activation(out=RB[ri][:, 0:N], in_=ps, func=AF.Relu),
                  reads=[("ps", bk)], writes=[("RB", ri)] + extra)
            P.add("vector", lambda e, ri=ri, fc=fc: e.tensor_tensor(out=A2T[ai][:, fc, 0:N], in0=RB[ri][:, 0:N], in1=RB[ri][:, 0:N], op=ALU.mult),
                  reads=[("RB", ri)], writes=[("A2T", ai, fc)] + extra)
            if pending and fc % 2 == 1:
                pop_pending(pending)

    def w2_step(si):
        b, G = steps[si]
        ai = si % 2
        last = (b == NBLK - 1)
        fins = []
        for i, (ti, pn) in enumerate(G["tiles"]):
            tb = nxt("tm", 2)
            for half in range(2):
                for fc in range(8):
                    P.add("tensor", lambda e, fc=fc, half=half, i=i, pn=pn, tb=tb: e.matmul(
                        out=psTM[tb][0:pn, half * 512:(half + 1) * 512], lhsT=A2T[ai][:, fc, i * 128:i * 128 + pn],
                        rhs=W2B[b % 2][:, fc, half * 512:(half + 1) * 512], start=(fc == 0), stop=(fc == 7)),
                        reads=[("A2T", ai, fc), ("W2B", b % 2)], writes=TMK[tb])
            x1 = X1[0:pn, ti, :]
            P.add("vector", lambda e, x1=x1, pn=pn, tb=tb: e.tensor_tensor(out=x1, in0=psTM[tb][0:pn, :], in1=x1, op=ALU.add),
                  reads=TMK[tb] + [("X1", ti)], writes=[("X1", ti)])
            if last:
                slot = nxt("slot", NSLOT)
                P.add("scalar", lambda e, x1=x1, pn=pn, slot=slot: e.activation(out=JUNK[0:pn, :], in_=x1, func=AF.Square,
                                                                                accum_out=ss[0:pn, slot:slot + 1]),
                      reads=[("X1", ti)], writes=[("JUNK",), ("ss", slot)])
                P.add("scalar", lambda e, pn=pn, slot=slot: e.activation(out=sd[0:pn, slot:slot + 1], in_=ss[0:pn, slot:slot + 1],
                                                                         func=AF.Sqrt, bias=EPS, scale=1.0 / D),
                      reads=[("ss", slot)], writes=[("sd", slot)])

                def fin(x1=x1, pn=pn, slot=slot, ti=ti, G=G):
                    P.add("vector", lambda e: e.reciprocal(out=rs[0:pn, slot:slot + 1], in_=sd[0:pn, slot:slot + 1]),
                          reads=[("sd", slot)], writes=[("rs", slot)])
                    P.add("vector", lambda e: e.scalar_tensor_tensor(
                        out=x1, in0=x1, scalar=rs[0:pn, slot:slot + 1], in1=FG[0:pn, :], op0=ALU.mult, op1=ALU.mult),
                        reads=[("X1", ti), ("rs", slot), ("FG",)], writes=[("X1", ti)])
                    if G["kind"] == "prompt":
                        P.dma("sync", lambda e: e.dma_start(out=yp[ti * 128:(ti + 1) * 128, :], in_=X1[:, ti, :]),
                              reads=[("X1", ti)], sem="out")
                    else:
                        for t in range(TS):
                            P.dma("sync", lambda e, t=t: e.dma_start(out=ys[:, t, :], in_=X1[t * 16:(t + 1) * 16, ti, :]),
                                  reads=[("X1", ti)], sem="out")
                fins.append(fin)
                if len(fins) > 1:
                    fins.pop(0)()
        while fins:
            fins.pop(0)()

    load_block(0)
    ns = len(steps)
    w1_step(0, pending)
    pop_pending(pending, 8)
    load_block(1)
    P.dma("sync", lambda e: e.dma_start(out=FG, in_=fg.partition_broadcast(128)), writes=[("FG",)] + P.keys_of(*MIXERS), sem="fgl")
    for si in range(ns):
        if si + 1 < ns:
            w1_step(si + 1)
        w2_step(si)
        b, G = steps[si]
        if G is bgroups[-1] and b + 2 < NBLK:
            load_block(b + 2)

    P.emit(nc, final_waits=["out", "o_nps", "o_ncs", "o_npp", "o_ncp"])
    return nc


_NC_CACHE = {}


def _get_nc():
    if "nc" not in _NC_CACHE:
        _NC_CACHE["nc"] = build_nc()
    return _NC_CACHE["nc"]


def kernel(x_prompt, x_sample, state_pool, state_conv, meta_tokens, norm1_g, w_in, pool_w, pool_scale, conv_w,
           w_out, norm2_g, w1, w2, final_g):
    f = lambda a: np.ascontiguousarray(np.asarray(a, dtype=np.float32))
    x_prompt = f(x_prompt); x_sample = f(x_sample); state_pool = f(state_pool); state_conv = f(state_conv)
    shared = {
        "meta": f(meta_tokens), "g1": f(norm1_g).reshape(D), "w_in": f(w_in).reshape(D, 2048),
        "pool_w": f(pool_w).reshape(4, 128, 128), "pool_scale": f(pool_scale).reshape(512),
        "conv_w": f(conv_w).reshape(3, 512), "w_out": f(w_out).reshape(D, D), "g2": f(norm2_g).reshape(D),
        "w1": f(w1).reshape(D, 4096), "w2": f(w2).reshape(4096, D), "fg": f(final_g).reshape(D),
    }
    in_maps = []
    for c in range(N_CORES):
        m = dict(shared)
        m["xp"] = x_prompt[c]
        m["xs"] = x_sample[c * NSEQ_S:(c + 1) * NSEQ_S]
        m["sp"] = state_pool[0, c * NSEQ_S:(c + 1) * NSEQ_S]
        m["sc"] = state_conv[0, c * NSEQ_S:(c + 1) * NSEQ_S]
        in_maps.append(m)
    nc = build_nc()
    res = run_bass_kernel_spmd(nc, in_maps, core_ids=list(range(N_CORES)))
    R = res.results
    y_prompt = np.stack([R[c]["yp"] for c in range(N_CORES)], axis=0)
    y_sample = np.concatenate([R[c]["ys"] for c in range(N_CORES)], axis=0)
    npp = np.stack([R[c]["npp"] for c in range(N_CORES)], axis=0)[None]
    ncp = np.stack([R[c]["ncp"] for c in range(N_CORES)], axis=0)[None]
    nps = np.concatenate([R[c]["nps"] for c in range(N_CORES)], axis=0)[None]
    ncs = np.concatenate([R[c]["ncs"] for c in range(N_CORES)], axis=0)[None]
    return (y_prompt.astype(np.float32), y_sample.astype(np.float32), npp.astype(np.float32), ncp.astype(np.float32),
            nps.astype(np.float32), ncs.astype(np.float32))
```

```python
import contextlib
import numpy as np
import concourse.bass as bass
import concourse.mybir as mybir
from concourse.bass_utils import run_bass_kernel_spmd

F32 = mybir.dt.float32
BF16 = mybir.dt.bfloat16
U8 = mybir.dt.uint8
ALU = mybir.AluOpType
AF = mybir.ActivationFunctionType

N_CORES = 8
D = 1024
SEQ = 2048
NMETA = 16
NSEQ_S = 16
TS = 4
EPS = 1e-6
POOL_WIN = (2, 4, 8, 16)

COMPUTE = ("tensor", "vector", "scalar", "gpsimd")
QUEUES = ("sync", "tensor", "vector", "scalar", "gpsimd")


class _Op:
    __slots__ = ("eng", "fn", "deps", "signal", "count", "dma_sem", "dma_cnt", "idx")


class Prog:
    def __init__(self):
        self.streams = {q: [] for q in QUEUES}
        self.last_w = {}
        self.readers = {}
        self.bykey0 = {}
        self.dma_counts = {}

    def keys_of(self, *bufnames):
        out = []
        for b in bufnames:
            out.extend(self.bykey0.get(b, ()))
        return out

    def _collect(self, eng, reads, writes, is_dma):
        deps = set()
        for r in reads:
            ev = self.last_w.get(r)
            if ev is not None and (ev[0] == "D" or is_dma or ev[1] != eng or eng != "tensor"):
                deps.add(ev)
        for w in writes:
            ev = self.last_w.get(w)
            if ev is not None and (ev[0] == "D" or is_dma or ev[1] != eng or eng != "tensor"):
                deps.add(ev)
            for ev in self.readers.get(w, ()):
                if ev[0] == "D" or is_dma or ev[1] != eng or eng != "tensor":
                    deps.add(ev)
        return deps

    def _register(self, ev, reads, writes):
        for k in list(reads) + list(writes):
            s = self.bykey0.setdefault(k[0] if isinstance(k, tuple) else k, set())
            s.add(k)
        for r in reads:
            self.readers.setdefault(r, []).append(ev)
        for w in writes:
            self.last_w[w] = ev
            self.readers[w] = []

    def add(self, eng, fn, reads=(), writes=()):
        op = _Op()
        op.eng = eng; op.fn = fn; op.signal = False; op.count = None
        op.dma_sem = None; op.dma_cnt = None
        op.deps = self._collect(eng, reads, writes, False)
        op.idx = len(self.streams[eng])
        self.streams[eng].append(op)
        self._register(("E", eng, op.idx), reads, writes)
        return op

    def dma(self, queue, fn, reads=(), writes=(), sem="dma"):
        op = _Op()
        op.eng = queue; op.fn = fn; op.signal = False; op.count = None
        op.deps = self._collect(queue, reads, writes, True)
        cnt = self.dma_counts.get(sem, 0) + 16
        self.dma_counts[sem] = cnt
        op.dma_sem = sem; op.dma_cnt = cnt
        op.idx = len(self.streams[queue])
        self.streams[queue].append(op)
        self._register(("D", sem, cnt), reads, writes)
        return op

    def emit(self, nc, final_waits=()):
        streams = self.streams
        for q in QUEUES:
            for op in streams[q]:
                for ev in op.deps:
                    if ev[0] == "E":
                        streams[ev[1]][ev[2]].signal = True
        for q in QUEUES:
            c = 0
            for op in streams[q]:
                if op.dma_sem is None and op.signal:
                    c += 1
                    op.count = c
        with contextlib.ExitStack() as st:
            esem = {e: st.enter_context(nc.semaphore("s_" + e)) for e in COMPUTE}
            dsem = {n: st.enter_context(nc.semaphore("d_" + n)) for n in self.dma_counts}
            block = st.enter_context(nc.Block())

            def run(q, eng):
                seen = {}
                for op in streams[q]:
                    waits = {}
                    for ev in op.deps:
                        if ev[0] == "E":
                            s = esem[ev[1]]; v = streams[ev[1]][ev[2]].count; key = "E" + ev[1]
                        else:
                            s = dsem[ev[1]]; v = ev[2]; key = "D" + ev[1]
                        if seen.get(key, 0) >= v:
                            continue
                        if key not in waits or waits[key][1] < v:
                            waits[key] = (s, v)
                    for key, (s, v) in waits.items():
                        eng.wait_ge(s, v)
                        seen[key] = v
                    ins = op.fn(eng)
                    if op.dma_sem is not None:
                        ins.then_inc(dsem[op.dma_sem], 16)
                    elif op.signal:
                        ins.then_inc(esem[q], 1)
                if q == "sync":
                    for n in final_waits:
                        eng.wait_ge(dsem[n], self.dma_counts[n])

            @block.sync
            def _(e):
                run("sync", e)

            @block.tensor
            def _(e):
                run("tensor", e)

            @block.vector
            def _(e):
                run("vector", e)

            @block.scalar
            def _(e):
                run("scalar", e)

            @block.gpsimd
            def _(e):
                run("gpsimd", e)


def build_nc():
    nc = bass.Bass("TRN2", target_bir_lowering=False)

    def din(name, shape):
        return nc.dram_tensor(name, list(shape), F32, kind="ExternalInput").ap()

    def dout(name, shape):
        return nc.dram_tensor(name, list(shape), F32, kind="ExternalOutput").ap()

    xp = din("xp", [SEQ, D]); xs = din("xs", [NSEQ_S, TS, D])
    sp = din("sp", [NSEQ_S, 15, 512]); sc = din("sc", [NSEQ_S, 2, 512])
    meta = din("meta", [NMETA, D]); g1 = din("g1", [D]); w_in = din("w_in", [D, 2048])
    pool_w = din("pool_w", [4, 128, 128]); pool_scale = din("pool_scale", [512]); conv_w = din("conv_w", [3, 512])
    w_out = din("w_out", [D, D]); g2 = din("g2", [D]); w1 = din("w1", [D, 4096]); w2 = din("w2", [4096, D])
    fg = din("fg", [D])
    yp = dout("yp", [SEQ, D]); ys = dout("ys", [NSEQ_S, TS, D])
    npp = dout("npp", [15, 512]); ncp = dout("ncp", [2, 512])
    nps = dout("nps", [NSEQ_S, 15, 512]); ncs = dout("ncs", [NSEQ_S, 2, 512])

    TOTAL = 212736
    big = nc.alloc_sbuf_tensor("big", [128, TOTAL], U8)
    cur = [0]

    def carve(nbytes, dt, pattern=None, at=None, **kw):
        if at is None:
            off = cur[0]
            cur[0] += (nbytes + 63) // 64 * 64
        else:
            off = at
        assert off + nbytes <= TOTAL, (off, nbytes)
        v = big[:, off:off + nbytes].bitcast(dt)
        if pattern:
            v = v.rearrange(pattern, **kw)
        return v

    NT = 17
    NCOL = SEQ + 64
    X1 = carve(NT * D * 4, F32, "p (t d) -> p t d", t=NT)
    H2T = carve(8 * NCOL * 2, BF16, "p (k n) -> p k n", k=8)
    identb = carve(128 * 2, BF16)
    identf = carve(128 * 4, F32)
    CONST = carve(32 * 4, F32)
    g1T = CONST[:, 0:8]; g2T = CONST[:, 8:16]; psc = CONST[:, 16:20]
    cw = CONST[:, 20:32].rearrange("p (j c) -> p j c", j=3)
    PSC = carve(4 * 4, F32)
    NSLOT = 16
    ss = carve(NSLOT * 4, F32); sd = carve(NSLOT * 4, F32); rs = carve(NSLOT * 4, F32)
    R0 = cur[0]
    WIN = carve(8 * 2048 * 2, BF16, "p (k e) -> p k e", k=8)
    WOUT = carve(8 * 1024 * 2, BF16, "p (k e) -> p k e", k=8)
    R1 = cur[0]
    PW = carve(4 * 128 * 2, BF16, "p (g d) -> p g d", g=4)
    HN = [carve(D * 2, BF16) for _ in range(2)]
    JA = carve(D * 2, BF16)
    HNT = carve(8 * 512 * 2, BF16, "p (k n) -> p k n", k=8)
    UL = 528
    UB = carve(4 * UL * 4, F32, "p (c n) -> p c n", c=4)
    CGB = [carve(512 * 4, F32) for _ in range(2)]
    ZL = 516
    ZB = carve(4 * ZL * 4, F32, "p (c n) -> p c n", c=4)
    PT = [carve(UL * 4, F32) for _ in range(2)]
    CV = [carve(512 * 4, F32) for _ in range(2)]
    DT = carve(4 * 512 * 2, BF16, "p (c n) -> p c n", c=4)
    MIX = carve(8 * 512 * 2, BF16, "p (k n) -> p k n", k=8)
    endA = cur[0]
    W1B = [carve(8 * 1024 * 2, BF16, "p (k f) -> p k f", k=8, at=R0 + i * 16384) for i in range(2)]
    W2B = [carve(8 * 1024 * 2, BF16, "p (c d) -> p c d", c=8, at=R0 + 32768 + i * 16384) for i in range(2)]
    o = R0 + 65536
    A2T = [carve(8 * 512 * 2, BF16, "p (c n) -> p c n", c=8, at=o + i * 8192) for i in range(2)]
    o += 16384
    RB = [carve(512 * 4, F32, at=o + i * 2048) for i in range(2)]
    o += 4096
    FG = carve(D * 4, F32, at=o); o += 4096
    JUNK = carve(D * 2, BF16, at=o); o += 2048
    assert o <= TOTAL and endA <= TOTAL, (o, endA)
    W2B1_ALIAS = ("PW", "HN", "JA", "HNT", "UB")
    MIXERS = ("UB", "CGB", "ZB", "PT", "CV")
    assert (endA - R0) >= 0 and (R0 + 92160) <= (R0 + 93632)

    def h2t_f32(k, lo, n):
        return H2T[:, k, 512 + lo // 2:512 + lo // 2 + 2 * n].bitcast(F32)

    def h2t_bf(k, lo, n):
        return H2T[:, k, 512 + lo // 2:512 + lo // 2 + n]

    UBS0 = h2t_f32(0, 0, 2 * 304).rearrange("p (c n) -> p c n", c=2)
    UBS1 = h2t_f32(1, 0, 2 * 304).rearrange("p (c n) -> p c n", c=2)
    ZBS = h2t_f32(2, 0, 4 * 96).rearrange("p (c n) -> p c n", c=4)
    CVS = [h2t_f32(2, 1536 + i * 256, 64) for i in range(4)]
    PTS = [h2t_f32(3, i * 1216, 304) for i in range(2)]
    CGBS = [h2t_f32(4, i * 256, 64) for i in range(2)]
    DTS = h2t_bf(4, 512, 256).rearrange("p (c n) -> p c n", c=4)
    MIXS = h2t_bf(4, 1024, 512).rearrange("p (k n) -> p k n", k=8)
    HNTS = h2t_bf(4, 2048, 512).rearrange("p (k n) -> p k n", k=8)
    SL = [h2t_f32(5 + i, 0, 512) for i in range(3)]

    class UBsplit:
        def __getitem__(self, idx):
            p, c, n = idx
            if isinstance(c, slice):
                raise TypeError
            return (UBS0 if c < 2 else UBS1)[p, c % 2, n]

    BM = dict(UB=UB, ZB=ZB, CGB=CGB, CV=CV, PT=PT, DT=DT, MIX=MIX, HNT=HNT, n="")
    BS = dict(UB=UBsplit(), ZB=ZBS, CGB=CGBS, CV=CVS, PT=PTS, DT=DTS, MIX=MIXS, HNT=HNTS, n="S")
    S_PRIVATE = ("UBS", "ZBS", "CGBS", "CVS", "PTS", "DTS", "MIXS", "HNTS", "SL")

    psall = nc.alloc_psum_tensor("psall", [128, 4096], F32).ap()
    NFM = 4
    psFM = [psall[:, i * 512:(i + 1) * 512] for i in range(NFM)]
    psTM = [psall[:, 2048:3072], psall[:, 3072:4096]]
    psT = [psall[:, 3072:3584].bitcast(BF16), psall[:, 3584:4096].bitcast(BF16),
           psall[:, 2048:2560].bitcast(BF16), psall[:, 2560:3072].bitcast(BF16)]
    TMK = [[("ps", 4), ("ps", 5)], [("ps", 6), ("ps", 7)]]
    TK = [("ps", 6), ("ps", 7), ("ps", 4), ("ps", 5)]

    P = Prog()
    cnt = {"fm": 0, "tm": 0, "t": 0, "slot": 0, "hn": 0}

    def nxt(name, mod):
        v = cnt[name] % mod
        cnt[name] += 1
        return v

    groups = []
    groups.append(dict(name="meta", N=32, tiles=[(15, 32)], stride=1, H=16, Hz=2, col0=None, kind="meta"))
    groups.append(dict(name="P0", N=512, tiles=[(i, 128) for i in range(4)], stride=1, H=16, Hz=2, col0=0, kind="prompt", last=False))
    groups.append(dict(name="S", N=64, tiles=[(16, 64)], stride=16, H=240, Hz=32, col0=SEQ, kind="sample"))
    for g in range(1, 4):
        groups.append(dict(name="P%d" % g, N=512, tiles=[(4 * g + i, 128) for i in range(4)], stride=1, H=16, Hz=2,
                           col0=512 * g, kind="prompt", last=(g == 3)))

    for G in groups:
        G["B"] = BS if G["kind"] == "sample" else BM

    spf = sp.rearrange("s r f -> (s r) f")
    scf = sc.rearrange("s r f -> (s r) f")

    def load_state():
        P.dma("sync", lambda e: e.dma_start(out=SL[0][0:120, :], in_=spf[0:120, :]), writes=[("SL", 0)], sem="stateA")
        P.dma("sync", lambda e: e.dma_start(out=SL[1][0:120, :], in_=spf[120:240, :]), writes=[("SL", 1)], sem="stateB")
        P.dma("sync", lambda e: e.dma_start(out=SL[2][0:32, :], in_=scf), writes=[("SL", 2)], sem="stateC")

    CST = PT[1]
    P.dma("gpsimd", lambda e: e.dma_start(out=CST[0:8, 0:128], in_=g1.rearrange("(k p) -> k p", p=128)), writes=[("CSTsub", 0)], sem="cst")
    P.dma("gpsimd", lambda e: e.dma_start(out=CST[8:16, 0:128], in_=g2.rearrange("(k p) -> k p", p=128)), writes=[("CSTsub", 1)], sem="cst")
    P.dma("gpsimd", lambda e: e.dma_start(out=CST[16:20, 0:128], in_=pool_scale.rearrange("(g p) -> g p", p=128)), writes=[("CSTsub", 2)], sem="cst")
    P.dma("gpsimd", lambda e: e.dma_start(out=CST[20:32, 0:128], in_=conv_w.rearrange("j (c p) -> (j c) p", p=128)), writes=[("CSTsub", 3)], sem="cst")
    P.dma("sync", lambda e: e.dma_start(out=X1[0:16, 15, :], in_=meta), writes=[("X1sub", 15, 0)], sem="xmeta")
    P.dma("sync", lambda e: e.dma_start(out=X1[16:32, 15, :], in_=meta), writes=[("X1sub", 15, 1)], sem="xmeta")

    def load_xp(g, gate=()):
        P.dma("sync", lambda e, g=g: e.dma_start(out=X1[:, 4 * g:4 * g + 4, :],
                                                  in_=xp[512 * g:512 * (g + 1), :].rearrange("(t p) d -> p t d", p=128)),
              reads=list(gate), writes=[("X1", 4 * g + i) for i in range(4)], sem="xP%d" % g)

    load_xp(0)

    qorder = [(0, 0), (2, 1024), (3, 1536), (1, 512)]
    for qi, e0 in qorder:
        P.dma("gpsimd", lambda e, e0=e0: e.dma_start(out=WIN[:, :, e0:e0 + 512],
                                                      in_=w_in[:, e0:e0 + 512].rearrange("(k p) e -> p k e", p=128)),
              writes=[("WIN", qi)], sem="win%d" % qi)
    P.dma("gpsimd", lambda e: e.dma_start(out=PW, in_=pool_w.rearrange("g c d -> c g d")), writes=[("PW",)], sem="pw")
    P.dma("gpsimd", lambda e: e.dma_start(out=WOUT, in_=w_out.rearrange("(k p) e -> p k e", p=128)), writes=[("WOUT",)], sem="wout")

    P.add("gpsimd", lambda e: e.memset(identf, 0.0), writes=[("identf",)])
    P.add("gpsimd", lambda e: e.affine_select(out=identf, in_=identf, compare_op=ALU.not_equal, fill=1.0, base=0,
                                               pattern=[[-1, 128]], channel_multiplier=1),
          reads=[("identf",)], writes=[("identf",)])
    P.add("vector", lambda e: e.tensor_copy(out=identb, in_=identf), reads=[("identf",)], writes=[("identb",)])
    P.add("tensor", lambda e: e.transpose(out=psFM[0][:, 0:32], in_=CST[0:32, 0:128], identity=identf[0:32, 0:32]),
          reads=[("PT", 1), ("identf",)] + [("CSTsub", i) for i in range(4)], writes=[("ps", 0)])
    P.add("vector", lambda e: e.tensor_copy(out=CONST, in_=psFM[0][:, 0:32]), reads=[("ps", 0)], writes=[("CONST",)])
    cnt["fm"] = 1
    CK = ("CONST",)
    for c in range(4):
        P.add("vector", lambda e, c=c: e.tensor_scalar(out=PSC[:, c:c + 1], in0=psc[:, c:c + 1], scalar1=1.0 / POOL_WIN[c], scalar2=None,
                                                        op0=ALU.mult), reads=[CK], writes=[("PSC",)])

    def stats(src, pn, src_keys, junk, junk_key):
        slot = nxt("slot", NSLOT)
        P.add("scalar", lambda e: e.activation(out=junk[0:pn, :], in_=src, func=AF.Square, accum_out=ss[0:pn, slot:slot + 1]),
              reads=src_keys, writes=[junk_key, ("ss", slot)])
        P.add("scalar", lambda e: e.activation(out=sd[0:pn, slot:slot + 1], in_=ss[0:pn, slot:slot + 1], func=AF.Sqrt,
                                               bias=EPS, scale=1.0 / D),
              reads=[("ss", slot)], writes=[("sd", slot)])
        P.add("vector", lambda e: e.reciprocal(out=rs[0:pn, slot:slot + 1], in_=sd[0:pn, slot:slot + 1]),
              reads=[("sd", slot)], writes=[("rs", slot)])
        return slot

    def scale_to_hn(src, pn, src_keys, slot, buf=None, bkey=None, pool=False):
        if buf is None:
            hi = nxt("hn", 2)
            buf = HN[hi]; bkey = ("HN", hi)
        hn = buf[0:pn, :]
        if pool:
            P.add("gpsimd", lambda e: e.tensor_scalar(out=hn, in0=src, scalar1=rs[0:pn, slot:slot + 1], scalar2=0.0,
                                                       op0=ALU.mult, op1=ALU.add),
                  reads=src_keys + [("rs", slot)], writes=[bkey])
        else:
            P.add("scalar", lambda e: e.activation(out=hn, in_=src, func=AF.Copy, scale=rs[0:pn, slot:slot + 1]),
                  reads=src_keys + [("rs", slot)], writes=[bkey])
        return hn, bkey

    def T1_prescale_ext(G):
        G.setdefault("hn1", {})
        for i in (2, 3):
            ti, pn = G["tiles"][i]
            if i not in G["hn1"]:
                G["hn1"][i] = scale_to_hn(X1[0:pn, ti, :], pn, xkeys(G, ti), G["slots1"][i],
                                          buf=CGB[i - 2].bitcast(BF16), bkey=("CGB", i - 2), pool=True)

    def transpose_to_fm(hn, hn_key, pn, gT, dst, dst_keys):
        tb = nxt("t", 4)
        for k in range(8):
            P.add("tensor", lambda e, k=k: e.transpose(out=psT[tb][:, k * 128:k * 128 + pn], in_=hn[:, k * 128:(k + 1) * 128],
                                                       identity=identb[0:pn, 0:pn]),
                  reads=[hn_key, ("identb",)], writes=[TK[tb]])
        src = psT[tb].rearrange("p (k n) -> p k n", k=8)[:, :, 0:pn]
        P.add("vector", lambda e: e.tensor_tensor(out=dst, in0=src, in1=gT.unsqueeze(2).to_broadcast([128, 8, pn]), op=ALU.mult),
              reads=[TK[tb], CK], writes=dst_keys)

    def fm_matmul(lhs_fn, rhs_fn, nk, N, reads):
        b = nxt("fm", NFM)
        for k in range(nk):
            P.add("tensor", lambda e, k=k: e.matmul(out=psFM[b][:, 0:N], lhsT=lhs_fn(k), rhs=rhs_fn(k), start=(k == 0), stop=(k == nk - 1)),
                  reads=reads, writes=[("ps", b)])
        return b, psFM[b][:, 0:N]

    def xkeys(G, ti):
        if G["kind"] == "sample":
            return [("X1", ti)] + [("X1sub", 16, t) for t in range(TS)]
        if G["kind"] == "meta":
            return [("X1", ti), ("X1sub", 15, 0), ("X1sub", 15, 1)]
        return [("X1", ti)]

    def pre_stats1(G, tiles=None):
        G.setdefault("slots1", {})
        for i, (ti, pn) in enumerate(G["tiles"]):
            if (tiles is not None and i not in tiles) or i in G["slots1"]:
                continue
            G["slots1"][i] = stats(X1[0:pn, ti, :], pn, xkeys(G, ti), JA, ("JA",))

    HM = MIX[:, 7, 0:256].rearrange("p (k n) -> p k n", k=8)

    def hnt_key(G, i):
        return ("MIX", 7) if G["kind"] == "meta" else ("HNT" + G["B"]["n"], i)

    def T1_prescale(G, n=2):
        G.setdefault("hn1", {})
        for i, (ti, pn) in enumerate(G["tiles"][:n]):
            if i not in G["hn1"]:
                G["hn1"][i] = scale_to_hn(X1[0:pn, ti, :], pn, xkeys(G, ti), G["slots1"][i])

    def T1(G):
        B = G["B"]; sfx = B["n"]
        UB = B["UB"]; ZB = B["ZB"]; CGB = B["CGB"]; CV = B["CV"]; PT = B["PT"]; DT = B["DT"]; MIX = B["MIX"]; HNT = B["HNT"]
        KUB = "UB" + sfx; KZB = "ZB" + sfx; KCGB = "CGB" + sfx; KCV = "CV" + sfx; KPT = "PT" + sfx; KDT = "DT" + sfx
        KMIX = "MIX" + sfx; KHNT = "HNT" + sfx; NCV = len(CV)
        G.setdefault("hn1", {})
        for i, (ti, pn) in enumerate(G["tiles"]):
            if i not in G["hn1"]:
                G["hn1"][i] = scale_to_hn(X1[0:pn, ti, :], pn, xkeys(G, ti), G["slots1"][i])
            hn, hk = G["hn1"][i]
            dst = HM[:, :, 0:pn] if G["kind"] == "meta" else HNT[:, :, i * 128:i * 128 + pn]
            transpose_to_fm(hn, hk, pn, g1T, dst, [hnt_key(G, i)])
            if i + 2 < len(G["tiles"]):
                T1_prescale(G, i + 3)

    class T2Item:
        def __init__(self, x1, pn, ti, slot, c0):
            self.x1 = x1; self.pn = pn; self.ti = ti; self.slot = slot; self.c0 = c0; self.hn = None

        def scale(self):
            if self.hn is None:
                self.hn = scale_to_hn(self.x1, self.pn, [("X1", self.ti)], self.slot)

        def trans(self):
            self.scale()
            hn, hk = self.hn
            extra = P.keys_of(*S_PRIVATE) if 4 <= self.ti < 16 else []
            transpose_to_fm(hn, hk, self.pn, g2T, H2T[:, :, self.c0:self.c0 + self.pn], [("H2T", self.ti)] + extra)

    def pop_pending(pending, n=1):
        for _ in range(n):
            if pending:
                it = pending.pop(0)
                it.trans()
                if len(pending) >= 2:
                    pending[1].scale()
                elif len(pending) == 1:
                    pending[0].scale()

    def P2(G, pending, nxtG, deferred=None, stages=("in", "u", "cgh", "rest")):
        B = G["B"]; sfx = B["n"]
        UB = B["UB"]; ZB = B["ZB"]; CGB = B["CGB"]; CV = B["CV"]; PT = B["PT"]; DT = B["DT"]; MIX = B["MIX"]; HNT = B["HNT"]
        KUB = "UB" + sfx; KZB = "ZB" + sfx; KCGB = "CGB" + sfx; KCV = "CV" + sfx; KPT = "PT" + sfx; KDT = "DT" + sfx
        KMIX = "MIX" + sfx; KHNT = "HNT" + sfx; NCV = len(CV)
        N = G["N"]; s = G["stride"]; H = G["H"]; Hz = G["Hz"]; kind = G["kind"]
        zo = G.get("zo", 0)
        L = H + N
        mixing = kind != "meta"
        if kind == "sample" and "in" in stages:
            for c in range(4):
                for half in range(2):
                    b = nxt("fm", NFM)
                    P.add("tensor", lambda e, c=c, half=half, b=b: e.transpose(
                        out=psFM[b][:, 0:120], in_=SL[half][0:120, c * 128:(c + 1) * 128], identity=identf[0:120, 0:120]),
                        reads=[("SL", half), ("identf",)], writes=[("ps", b)])
                    dst = UB[:, c, 0:240].rearrange("p (r s) -> p s r", s=16)[:, half * 8:(half + 1) * 8, :]
                    P.add("scalar", lambda e, dst=dst, b=b: e.copy(out=dst, in_=psFM[b][:, 0:120].rearrange("p (s r) -> p s r", r=15)),
                          reads=[("ps", b)], writes=[(KUB, c, "halo")])
                b = nxt("fm", NFM)
                P.add("tensor", lambda e, c=c, b=b: e.transpose(out=psFM[b][:, 0:32], in_=SL[2][0:32, c * 128:(c + 1) * 128],
                                                                identity=identf[0:32, 0:32]),
                      reads=[("SL", 2), ("identf",)], writes=[("ps", b)])
                dstz = ZB[:, c, 0:32].rearrange("p (r s) -> p s r", s=16)
                P.add("scalar", lambda e, dstz=dstz, b=b: e.copy(out=dstz, in_=psFM[b][:, 0:32].rearrange("p (s r) -> p s r", r=2)),
                      reads=[("ps", b)], writes=[(KZB, c, "halo")])
        hnt_keys = [hnt_key(G, i) for i in range(len(G["tiles"]))]
        RH = HM if kind == "meta" else HNT

        def win_mm(e0, qi):
            return fm_matmul(lambda k: WIN[:, k, e0:e0 + 128], lambda k: RH[:, k, 0:N], 8, N, [("WIN", qi)] + hnt_keys)

        dstate = {}

        def pool_adds(c):
            w = POOL_WIN[c]
            u = UB[:, c, :]
            lo = H - (w - 2) * s
            cur_src, cur_keys = u, [(KUB, c, "halo"), (KUB, c, "new")]
            step = s
            bi = 0
            for lev in range({2: 1, 4: 2, 8: 3, 16: 4}[w]):
                dstb = PT[bi]
                P.add("gpsimd", lambda e, dstb=dstb, cur_src=cur_src, a0=lo, step=step: e.tensor_tensor(
                    out=dstb[:, a0:L], in0=cur_src[:, a0:L], in1=cur_src[:, a0 - step:L - step], op=ALU.add),
                    reads=cur_keys, writes=[(KPT, bi)])
                cur_src, cur_keys = dstb, [(KPT, bi)]
                lo += 2 * step
                step *= 2
                bi ^= 1
            tmp = PT[bi]
            P.add("gpsimd", lambda e: e.tensor_scalar(out=tmp[:, H:L], in0=u[:, H:L], scalar1=-float(w), scalar2=0.0, op0=ALU.mult, op1=ALU.add),
                  reads=[(KUB, c, "new")], writes=[(KPT, bi)])
            P.add("gpsimd", lambda e: e.tensor_tensor(out=DT[:, c, 0:N], in0=cur_src[:, H:L], in1=tmp[:, H:L], op=ALU.add),
                  reads=cur_keys + [(KPT, bi)], writes=[(KDT, c)])

        def halo_u():
            P.add("gpsimd", lambda e: e.tensor_copy(out=UB[:, :, 0:16], in_=UB[:, :, L - 16:L]),
                  reads=[(KUB, c, "new") for c in range(4)], writes=[(KUB, c, "halo") for c in range(4)])

        def halo_z():
            P.add("vector" if kind == "meta" else "gpsimd", lambda e: e.tensor_copy(out=ZB[:, :, 0:2], in_=ZB[:, :, Hz + N - 2:Hz + N]),
                  reads=[(KZB, c, "new") for c in range(4)], writes=[(KZB, c, "halo") for c in range(4)])

        if "u" in stages:
            for c in range(4):
                b, ps = win_mm(c * 128, 0)
                P.add("scalar", lambda e, c=c, ps=ps: e.copy(out=UB[:, c, H:L], in_=ps), reads=[("ps", b)], writes=[(KUB, c, "new")])
                if mixing:
                    pool_adds(c)
            pop_pending(pending)
            if kind == "meta":
                halo_u()

        def conv_chunk(c):
            z = ZB[:, c, :]
            zk = [(KZB, c, "halo"), (KZB, c, "new")]
            ci = c % NCV
            cv = CV[ci][:, 0:N]
            P.add("scalar", lambda e: e.activation(out=cv, in_=z[:, zo:zo + N], func=AF.Copy, scale=cw[:, 0, c:c + 1]),
                  reads=zk + [CK], writes=[(KCV, ci)])
            P.add("vector", lambda e: e.scalar_tensor_tensor(out=cv, in0=z[:, zo + s:zo + s + N], scalar=cw[:, 1, c:c + 1], in1=cv,
                                                             op0=ALU.mult, op1=ALU.add),
                  reads=zk + [CK, (KCV, ci)], writes=[(KCV, ci)])
            P.add("vector", lambda e: e.scalar_tensor_tensor(out=cv, in0=z[:, zo + 2 * s:zo + 2 * s + N], scalar=cw[:, 2, c:c + 1], in1=cv,
                                                             op0=ALU.mult, op1=ALU.add),
                  reads=zk + [CK, (KCV, ci)], writes=[(KCV, ci)])

        def cg_h(c):
            b, ps = win_mm(1024 + c * 128, 2)
            ci = c % 2
            P.add("scalar", lambda e, ps=ps: e.copy(out=CGB[ci][:, 0:N], in_=ps), reads=[("ps", b)], writes=[(KCGB, ci)])
            b2, ps2 = win_mm(1536 + c * 128, 3)
            P.add("vector", lambda e, ps2=ps2: e.tensor_tensor(out=ZB[:, c, Hz:Hz + N], in0=ps2, in1=CGB[ci][:, 0:N], op=ALU.mult),
                  reads=[("ps", b2), (KCGB, ci)], writes=[(KZB, c, "new")])

        def bg_chunk(c):
            ci = c % NCV
            cv = CV[ci][:, 0:N]
            b, ps = win_mm(512 + c * 128, 1)
            P.add("vector", lambda e, ps=ps: e.tensor_tensor(out=MIX[:, 4 + c, 0:N], in0=ps, in1=cv, op=ALU.mult),
                  reads=[("ps", b), (KCV, ci)], writes=[(KMIX, 4 + c)])

        if "cgh" in stages:
            nst = len(nxtG["tiles"]) if nxtG is not None else 0
            cg_h(0)
            if nst > 0:
                pre_stats1(nxtG, tiles=(0,))
            cg_h(1)
            if nst > 1:
                pre_stats1(nxtG, tiles=(1,))
            pop_pending(pending)
            early_conv = mixing and kind == "prompt" and "rest" in stages
            if early_conv:
                conv_chunk(0); conv_chunk(1)
            cg_h(2)
            if nst > 2:
                pre_stats1(nxtG, tiles=(2,))
            cg_h(3)
            if nst > 3:
                pre_stats1(nxtG, tiles=(3,))
            if deferred is not None:
                deferred()
            if "rest" in stages:
                pop_pending(pending)
            if kind == "meta":
                halo_z()
        else:
            early_conv = False
        if "rest" not in stages and "rest_a" not in stages and "rest_b" not in stages:
            return
        do_a = "rest" in stages or "rest_a" in stages
        do_b = "rest" in stages or "rest_b" in stages
        if mixing and do_a:
            if not early_conv:
                conv_chunk(0); conv_chunk(1)
            bg_chunk(0); bg_chunk(1)
            pop_pending(pending)
            if nxtG is not None and nxtG["kind"] == "prompt":
                T1_prescale(nxtG, 2 if not pending else 1)
            conv_chunk(2); conv_chunk(3)
            bg_chunk(2); bg_chunk(3)
        if do_a:
            pop_pending(pending, 8)
            if nxtG is not None and nxtG["kind"] == "prompt":
                T1_prescale(nxtG, 2)
                if kind == "prompt":
                    T1_prescale_ext(nxtG)
        if not do_b:
            return
        if kind == "sample" or (kind == "prompt" and G.get("last")):
            if kind == "sample":
                ucols = (H, L); zcols = (Hz, Hz + N); nr = 64
            else:
                ucols = (L - 16, L); zcols = (Hz + N - 2, Hz + N); nr = 16
            nz = zcols[1] - zcols[0]
            stg = SL if kind == "sample" else CGB
            kstg = "SL" if kind == "sample" else KCGB
            b = nxt("fm", NFM)
            for c in range(4):
                P.add("tensor", lambda e, c=c, b=b: e.transpose(out=psFM[b][0:nr, c * 128:(c + 1) * 128], in_=UB[:, c, ucols[0]:ucols[1]],
                                                                identity=identf),
                      reads=[(KUB, c, "new"), ("identf",)], writes=[("ps", b)])
            P.add("scalar", lambda e, b=b: e.copy(out=stg[0][0:nr, :], in_=psFM[b][0:nr, :]), reads=[("ps", b)], writes=[(kstg, 0)])
            b2 = nxt("fm", NFM)
            for c in range(4):
                P.add("tensor", lambda e, c=c, b2=b2: e.transpose(out=psFM[b2][0:nz, c * 128:(c + 1) * 128], in_=ZB[:, c, zcols[0]:zcols[1]],
                                                                  identity=identf),
                      reads=[(KZB, c, "new"), ("identf",)], writes=[("ps", b2)])
            P.add("scalar", lambda e, b2=b2: e.copy(out=stg[1][0:nz, :], in_=psFM[b2][0:nz, :]), reads=[("ps", b2)], writes=[(kstg, 1)])
            if kind == "sample":
                for t in range(TS):
                    P.dma("sync", lambda e, t=t: e.dma_start(out=nps[:, 11 + t, :], in_=stg[0][t * 16:(t + 1) * 16, :]),
                          reads=[(kstg, 0)], sem="o_nps")
                for t in (2, 3):
                    P.dma("sync", lambda e, t=t: e.dma_start(out=ncs[:, t - 2, :], in_=stg[1][t * 16:(t + 1) * 16, :]),
                          reads=[(kstg, 1)], sem="o_ncs")
            else:
                P.dma("sync", lambda e: e.dma_start(out=npp, in_=stg[0][1:16, :]), reads=[(kstg, 0)], sem="o_npp")
                P.dma("sync", lambda e: e.dma_start(out=ncp, in_=stg[1][0:2, :]), reads=[(kstg, 1)], sem="o_ncp")
        if kind == "prompt" and not G.get("last"):
            halo_u()
            halo_z()

    def P3(G):
        B = G["B"]; sfx = B["n"]
        UB = B["UB"]; ZB = B["ZB"]; CGB = B["CGB"]; CV = B["CV"]; PT = B["PT"]; DT = B["DT"]; MIX = B["MIX"]; HNT = B["HNT"]
        KUB = "UB" + sfx; KZB = "ZB" + sfx; KCGB = "CGB" + sfx; KCV = "CV" + sfx; KPT = "PT" + sfx; KDT = "DT" + sfx
        KMIX = "MIX" + sfx; KHNT = "HNT" + sfx; NCV = len(CV)
        N = G["N"]
        for c in range(4):
            b, ps = fm_matmul(lambda k, c=c: PW[:, c, :], lambda k, c=c: DT[:, c, 0:N], 1, N, [("PW",), (KDT, c)])
            P.add("scalar", lambda e, c=c, ps=ps: e.activation(out=MIX[:, c, 0:N], in_=ps, func=AF.Copy, scale=PSC[:, c:c + 1]),
                  reads=[("ps", b), ("PSC",)], writes=[(KMIX, c)])

    def P4(G, pending):
        B = G["B"]; sfx = B["n"]
        UB = B["UB"]; ZB = B["ZB"]; CGB = B["CGB"]; CV = B["CV"]; PT = B["PT"]; DT = B["DT"]; MIX = B["MIX"]; HNT = B["HNT"]
        KUB = "UB" + sfx; KZB = "ZB" + sfx; KCGB = "CGB" + sfx; KCV = "CV" + sfx; KPT = "PT" + sfx; KDT = "DT" + sfx
        KMIX = "MIX" + sfx; KHNT = "HNT" + sfx; NCV = len(CV)
        for i, (ti, pn) in enumerate(G["tiles"]):
            tb = nxt("tm", 2)
            korder = (4, 5, 6, 7, 0, 1, 2, 3)
            for half in range(2):
                for kk, k in enumerate(korder):
                    P.add("tensor", lambda e, k=k, kk=kk, half=half, i=i, pn=pn, tb=tb: e.matmul(
                        out=psTM[tb][0:pn, half * 512:(half + 1) * 512], lhsT=MIX[:, k, i * 128:i * 128 + pn],
                        rhs=WOUT[:, k, half * 512:(half + 1) * 512], start=(kk == 0), stop=(kk == 7)),
                        reads=[(KMIX, k), ("WOUT",)], writes=TMK[tb])
            x1 = X1[0:pn, ti, :]
            P.add("vector", lambda e, x1=x1, pn=pn, tb=tb: e.tensor_tensor(out=x1, in0=psTM[tb][0:pn, :], in1=x1, op=ALU.add),
                  reads=TMK[tb] + [("X1", ti)], writes=[("X1", ti)])
            slot = stats(x1, pn, [("X1", ti)], JA, ("JA",))
            c0 = G["col0"] + i * 128

            pending.append(T2Item(x1, pn, ti, slot, c0))
            if len(pending) <= 2:
                pending[-1].scale()
        for it in pending[:2]:
            it.scale()

    pending = []
    deferred = [None]
    Gm, G0, GS, G1 = groups[0], groups[1], groups[2], groups[3]
    load_state()
    for t in range(TS):
        P.dma("sync", lambda e, t=t: e.dma_start(out=X1[t * 16:(t + 1) * 16, 16, :], in_=xs[:, t, :]),
              writes=[("X1sub", 16, t)], sem="xS")
    pre_stats1(Gm)
    pre_stats1(G0)
    T1(Gm)
    T1(G0)
    pre_stats1(GS)
    T1_prescale(GS, 1)
    P2(Gm, [], None, stages=("u",))
    P2(G0, pending, G1, stages=("u",))
    T1(GS)
    P2(GS, [], None, stages=("in", "u"))
    load_xp(1, gate=[("HNT", 3)])
    P2(Gm, [], None, stages=("cgh",))
    P2(G0, pending, G1, stages=("cgh",))
    P2(GS, [], None, stages=("cgh",))
    P2(GS, [], None, stages=("rest_a",))
    P2(G0, pending, G1, stages=("rest",))
    deferred[0] = (lambda: P4(GS, pending))
    load_xp(2, gate=[("MIX", 7)])
    P.dma("sync", lambda e: e.dma_start(out=nps[:, 0:11, :], in_=sp[:, 4:15, :]), sem="out")
    T1(G1)
    P3(G0)
    P3(GS)
    P2(GS, [], None, stages=("rest_b",))
    P4(G0, pending)
    for gi in range(3, len(groups)):
        G = groups[gi]
        nxtG = groups[gi + 1] if gi + 1 < len(groups) else None
        dfr = None
        if deferred[0] is not None:
            dfr, deferred[0] = deferred[0], None
        P2(G, pending, nxtG, dfr)
        if gi == 3:
            load_xp(3, gate=[("MIX", 7)])
        if nxtG is not None:
            T1(nxtG)
        P3(G)
        P4(G, pending)

    bgroups = [g for g in groups if g["kind"] == "sample"] + [g for g in groups if g["kind"] == "prompt"]
    NBLK = 4

    def load_block(b):
        i = b % 2
        extra1 = P.keys_of("WIN") if b < 2 else []
        if b == 0:
            extra2 = P.keys_of("WOUT")
        elif b == 1:
            extra2 = P.keys_of(*W2B1_ALIAS)
        else:
            extra2 = []
        P.dma("gpsimd", lambda e: e.dma_start(out=W1B[i], in_=w1[:, b * 1024:(b + 1) * 1024].rearrange("(k p) f -> p k f", p=128)),
              writes=[("W1B", i)] + extra1, sem="w1b%d" % i)
        P.dma("gpsimd", lambda e: e.dma_start(out=W2B[i], in_=w2[b * 1024:(b + 1) * 1024, :].rearrange("(c p) d -> p c d", p=128)),
              writes=[("W2B", i)] + extra2, sem="w2b%d" % i)

    order0 = [bgroups[1], bgroups[0]] + bgroups[2:]
    steps = [(b, G) for b in range(NBLK) for G in (order0 if b == 0 else bgroups)]
    first_b_write = [True]

    def w1_step(si, pending=None):
        b, G = steps[si]
        N = G["N"]; col0 = G["col0"]; ai = si % 2
        tkeys = [("H2T", ti) for ti, _ in G["tiles"]]
        for fc in range(8):
            bk, ps = fm_matmul(lambda k, fc=fc: W1B[b % 2][:, k, fc * 128:(fc + 1) * 128], lambda k: H2T[:, k, col0:col0 + N], 8, N,
                               [("W1B", b % 2)] + tkeys)
            ri = fc % 2
            extra = P.keys_of(*MIXERS) if first_b_write[0] else []
            first_b_write[0] = False
            P.add("scalar", lambda e, ri=ri, ps=ps: e.activation(out=RB[ri][:, 0:N], in_=ps, func=AF.Relu),
                  reads=[("ps", bk)], writes=[("RB", ri)] + extra)
            P.add("vector", lambda e, ri=ri, fc=fc: e.tensor_tensor(out=A2T[ai][:, fc, 0:N], in0=RB[ri][:, 0:N], in1=RB[ri][:, 0:N], op=ALU.mult),
                  reads=[("RB", ri)], writes=[("A2T", ai, fc)] + extra)
            if pending and fc % 2 == 1:
                pop_pending(pending)

    def w2_step(si):
        b, G = steps[si]
        ai = si % 2
        last = (b == NBLK - 1)
        fins = []
        for i, (ti, pn) in enumerate(G["tiles"]):
            tb = nxt("tm", 2)
            for half in range(2):
                for fc in range(8):
                    P.add("tensor", lambda e, fc=fc, half=half, i=i, pn=pn, tb=tb: e.matmul(
                        out=psTM[tb][0:pn, half * 512:(half + 1) * 512], lhsT=A2T[ai][:, fc, i * 128:i * 128 + pn],
                        rhs=W2B[b % 2][:, fc, half * 512:(half + 1) * 512], start=(fc == 0), stop=(fc == 7)),
                        reads=[("A2T", ai, fc), ("W2B", b % 2)], writes=TMK[tb])
            x1 = X1[0:pn, ti, :]
            P.add("vector", lambda e, x1=x1, pn=pn, tb=tb: e.tensor_tensor(out=x1, in0=psTM[tb][0:pn, :], in1=x1, op=ALU.add),
                  reads=TMK[tb] + [("X1", ti)], writes=[("X1", ti)])
            if last:
                slot = nxt("slot", NSLOT)
                P.add("scalar", lambda e, x1=x1, pn=pn, slot=slot: e.activation(out=JUNK[0:pn, :], in_=x1, func=AF.Square,
                                                                                accum_out=ss[0:pn, slot:slot + 1]),
                      reads=[("X1", ti)], writes=[("JUNK",), ("ss", slot)])
                P.add("scalar", lambda e, pn=pn, slot=slot: e.activation(out=sd[0:pn, slot:slot + 1], in_=ss[0:pn, slot:slot + 1],
                                                                         func=AF.Sqrt, bias=EPS, scale=1.0 / D),
                      reads=[("ss", slot)], writes=[("sd", slot)])

                def fin(x1=x1, pn=pn, slot=slot, ti=ti, G=G):
                    P.add("vector", lambda e: e.reciprocal(out=rs[0:pn, slot:slot + 1], in_=sd[0:pn, slot:slot + 1]),
                          reads=[("sd", slot)], writes=[("rs", slot)])
                    P.add("vector", lambda e: e.scalar_tensor_tensor(
                        out=x1, in0=x1, scalar=rs[0:pn, slot:slot + 1], in1=FG[0:pn, :], op0=ALU.mult, op1=ALU.mult),
                        reads=[("X1", ti), ("rs", slot), ("FG",)], writes=[("X1", ti)])
                    if G["kind"] == "prompt":
                        P.dma("sync", lambda e: e.dma_start(out=yp[ti * 128:(ti + 1) * 128, :], in_=X1[:, ti, :]),
                              reads=[("X1", ti)], sem="out")
                    else:
                        for t in range(TS):
                            P.dma("sync", lambda e, t=t: e.dma_start(out=ys[:, t, :], in_=X1[t * 16:(t + 1) * 16, ti, :]),
                                  reads=[("X1", ti)], sem="out")
                fins.append(fin)
                if len(fins) > 1:
                    fins.pop(0)()
        while fins:
            fins.pop(0)()

    load_block(0)
    ns = len(steps)
    w1_step(0, pending)
    pop_pending(pending, 8)
    load_block(1)
    P.dma("sync", lambda e: e.dma_start(out=FG, in_=fg.partition_broadcast(128)), writes=[("FG",)] + P.keys_of(*MIXERS), sem="fgl")
    for si in range(ns):
        if si + 1 < ns:
            w1_step(si + 1)
        w2_step(si)
        b, G = steps[si]
        if G is bgroups[-1] and b + 2 < NBLK:
            load_block(b + 2)

    P.emit(nc, final_waits=["out", "o_nps", "o_ncs", "o_npp", "o_ncp"])
    return nc


_NC_CACHE = {}


def _get_nc():
    if "nc" not in _NC_CACHE:
        _NC_CACHE["nc"] = build_nc()
    return _NC_CACHE["nc"]


def kernel(x_prompt, x_sample, state_pool, state_conv, meta_tokens, norm1_g, w_in, pool_w, pool_scale, conv_w,
           w_out, norm2_g, w1, w2, final_g):
    f = lambda a: np.ascontiguousarray(np.asarray(a, dtype=np.float32))
    x_prompt = f(x_prompt); x_sample = f(x_sample); state_pool = f(state_pool); state_conv = f(state_conv)
    shared = {
        "meta": f(meta_tokens), "g1": f(norm1_g).reshape(D), "w_in": f(w_in).reshape(D, 2048),
        "pool_w": f(pool_w).reshape(4, 128, 128), "pool_scale": f(pool_scale).reshape(512),
        "conv_w": f(conv_w).reshape(3, 512), "w_out": f(w_out).reshape(D, D), "g2": f(norm2_g).reshape(D),
        "w1": f(w1).reshape(D, 4096), "w2": f(w2).reshape(4096, D), "fg": f(final_g).reshape(D),
    }
    in_maps = []
    for c in range(N_CORES):
        m = dict(shared)
        m["xp"] = x_prompt[c]
        m["xs"] = x_sample[c * NSEQ_S:(c + 1) * NSEQ_S]
        m["sp"] = state_pool[0, c * NSEQ_S:(c + 1) * NSEQ_S]
        m["sc"] = state_conv[0, c * NSEQ_S:(c + 1) * NSEQ_S]
        in_maps.append(m)
    nc = build_nc()
    res = run_bass_kernel_spmd(nc, in_maps, core_ids=list(range(N_CORES)))
    R = res.results
    y_prompt = np.stack([R[c]["yp"] for c in range(N_CORES)], axis=0)
    y_sample = np.concatenate([R[c]["ys"] for c in range(N_CORES)], axis=0)
    npp = np.stack([R[c]["npp"] for c in range(N_CORES)], axis=0)[None]
    ncp = np.stack([R[c]["ncp"] for c in range(N_CORES)], axis=0)[None]
    nps = np.concatenate([R[c]["nps"] for c in range(N_CORES)], axis=0)[None]
    ncs = np.concatenate([R[c]["ncs"] for c in range(N_CORES)], axis=0)[None]
    return (y_prompt.astype(np.float32), y_sample.astype(np.float32), npp.astype(np.float32), ncp.astype(np.float32),
            nps.astype(np.float32), ncs.astype(np.float32))
```

```python
import contextlib
import numpy as np
import concourse.bass as bass
import concourse.mybir as mybir
from concourse.bass_utils import run_bass_kernel_spmd

F32 = mybir.dt.float32
BF16 = mybir.dt.bfloat16
U8 = mybir.dt.uint8
ALU = mybir.AluOpType
AF = mybir.ActivationFunctionType

N_CORES = 8
D = 1024
SEQ = 2048
NMETA = 16
NSEQ_S = 16
TS = 4
EPS = 1e-6
POOL_WIN = (2, 4, 8, 16)

COMPUTE = ("tensor", "vector", "scalar", "gpsimd")
QUEUES = ("sync", "tensor", "vector", "scalar", "gpsimd")


class _Op:
    __slots__ = ("eng", "fn", "deps", "signal", "count", "dma_sem", "dma_cnt", "idx")


class Prog:
    def __init__(self):
        self.streams = {q: [] for q in QUEUES}
        self.last_w = {}
        self.readers = {}
        self.bykey0 = {}
        self.dma_counts = {}

    def keys_of(self, *bufnames):
        out = []
        for b in bufnames:
            out.extend(self.bykey0.get(b, ()))
        return out

    def _collect(self, eng, reads, writes, is_dma):
        deps = set()
        for r in reads:
            ev = self.last_w.get(r)
            if ev is not None and (ev[0] == "D" or is_dma or ev[1] != eng or eng != "tensor"):
                deps.add(ev)
        for w in writes:
            ev = self.last_w.get(w)
            if ev is not None and (ev[0] == "D" or is_dma or ev[1] != eng or eng != "tensor"):
                deps.add(ev)
            for ev in self.readers.get(w, ()):
                if ev[0] == "D" or is_dma or ev[1] != eng or eng != "tensor":
                    deps.add(ev)
        return deps

    def _register(self, ev, reads, writes):
        for k in list(reads) + list(writes):
            s = self.bykey0.setdefault(k[0] if isinstance(k, tuple) else k, set())
            s.add(k)
        for r in reads:
            self.readers.setdefault(r, []).append(ev)
        for w in writes:
            self.last_w[w] = ev
            self.readers[w] = []

    def add(self, eng, fn, reads=(), writes=()):
        op = _Op()
        op.eng = eng; op.fn = fn; op.signal = False; op.count = None
        op.dma_sem = None; op.dma_cnt = None
        op.deps = self._collect(eng, reads, writes, False)
        op.idx = len(self.streams[eng])
        self.streams[eng].append(op)
        self._register(("E", eng, op.idx), reads, writes)
        return op

    def dma(self, queue, fn, reads=(), writes=(), sem="dma"):
        op = _Op()
        op.eng = queue; op.fn = fn; op.signal = False; op.count = None
        op.deps = self._collect(queue, reads, writes, True)
        cnt = self.dma_counts.get(sem, 0) + 16
        self.dma_counts[sem] = cnt
        op.dma_sem = sem; op.dma_cnt = cnt
        op.idx = len(self.streams[queue])
        self.streams[queue].append(op)
        self._register(("D", sem, cnt), reads, writes)
        return op

    def emit(self, nc, final_waits=()):
        streams = self.streams
        for q in QUEUES:
            for op in streams[q]:
                for ev in op.deps:
                    if ev[0] == "E":
                        streams[ev[1]][ev[2]].signal = True
        for q in QUEUES:
            c = 0
            for op in streams[q]:
                if op.dma_sem is None and op.signal:
                    c += 1
                    op.count = c
        with contextlib.ExitStack() as st:
            esem = {e: st.enter_context(nc.semaphore("s_" + e)) for e in COMPUTE}
            dsem = {n: st.enter_context(nc.semaphore("d_" + n)) for n in self.dma_counts}
            block = st.enter_context(nc.Block())

            def run(q, eng):
                seen = {}
                for op in streams[q]:
                    waits = {}
                    for ev in op.deps:
                        if ev[0] == "E":
                            s = esem[ev[1]]; v = streams[ev[1]][ev[2]].count; key = "E" + ev[1]
                        else:
                            s = dsem[ev[1]]; v = ev[2]; key = "D" + ev[1]
                        if seen.get(key, 0) >= v:
                            continue
                        if key not in waits or waits[key][1] < v:
                            waits[key] = (s, v)
                    for key, (s, v) in waits.items():
                        eng.wait_ge(s, v)
                        seen[key] = v
                    ins = op.fn(eng)
                    if op.dma_sem is not None:
                        ins.then_inc(dsem[op.dma_sem], 16)
                    elif op.signal:
                        ins.then_inc(esem[q], 1)
                if q == "sync":
                    for n in final_waits:
                        eng.wait_ge(dsem[n], self.dma_counts[n])

            @block.sync
            def _(e):
                run("sync", e)

            @block.tensor
            def _(e):
                run("tensor", e)

            @block.vector
            def _(e):
                run("vector", e)

            @block.scalar
            def _(e):
                run("scalar", e)

            @block.gpsimd
            def _(e):
                run("gpsimd", e)


def build_nc():
    nc = bass.Bass("TRN2", target_bir_lowering=False)

    def din(name, shape):
        return nc.dram_tensor(name, list(shape), F32, kind="ExternalInput").ap()

    def dout(name, shape):
        return nc.dram_tensor(name, list(shape), F32, kind="ExternalOutput").ap()

    xp = din("xp", [SEQ, D]); xs = din("xs", [NSEQ_S, TS, D])
    sp = din("sp", [NSEQ_S, 15, 512]); sc = din("sc", [NSEQ_S, 2, 512])
    meta = din("meta", [NMETA, D]); g1 = din("g1", [D]); w_in = din("w_in", [D, 2048])
    pool_w = din("pool_w", [4, 128, 128]); pool_scale = din("pool_scale", [512]); conv_w = din("conv_w", [3, 512])
    w_out = din("w_out", [D, D]); g2 = din("g2", [D]); w1 = din("w1", [D, 4096]); w2 = din("w2", [4096, D])
    fg = din("fg", [D])
    yp = dout("yp", [SEQ, D]); ys = dout("ys", [NSEQ_S, TS, D])
    npp = dout("npp", [15, 512]); ncp = dout("ncp", [2, 512])
    nps = dout("nps", [NSEQ_S, 15, 512]); ncs = dout("ncs", [NSEQ_S, 2, 512])

    TOTAL = 212736
    big = nc.alloc_sbuf_tensor("big", [128, TOTAL], U8)
    cur = [0]

    def carve(nbytes, dt, pattern=None, at=None, **kw):
        if at is None:
            off = cur[0]
            cur[0] += (nbytes + 63) // 64 * 64
        else:
            off = at
        assert off + nbytes <= TOTAL, (off, nbytes)
        v = big[:, off:off + nbytes].bitcast(dt)
        if pattern:
            v = v.rearrange(pattern, **kw)
        return v

    NT = 17
    NCOL = SEQ + 64
    X1 = carve(NT * D * 4, F32, "p (t d) -> p t d", t=NT)
    H2T = carve(8 * NCOL * 2, BF16, "p (k n) -> p k n", k=8)
    identb = carve(128 * 2, BF16)
    identf = carve(128 * 4, F32)
    CONST = carve(32 * 4, F32)
    g1T = CONST[:, 0:8]; g2T = CONST[:, 8:16]; psc = CONST[:, 16:20]
    cw = CONST[:, 20:32].rearrange("p (j c) -> p j c", j=3)
    PSC = carve(4 * 4, F32)
    NSLOT = 16
    ss = carve(NSLOT * 4, F32); sd = carve(NSLOT * 4, F32); rs = carve(NSLOT * 4, F32)
    R0 = cur[0]
    WIN = carve(8 * 2048 * 2, BF16, "p (k e) -> p k e", k=8)
    WOUT = carve(8 * 1024 * 2, BF16, "p (k e) -> p k e", k=8)
    R1 = cur[0]
    PW = carve(4 * 128 * 2, BF16, "p (g d) -> p g d", g=4)
    HN = [carve(D * 2, BF16) for _ in range(2)]
    JA = carve(D * 2, BF16)
    HNT = carve(8 * 512 * 2, BF16, "p (k n) -> p k n", k=8)
    UL = 528
    UB = carve(4 * UL * 4, F32, "p (c n) -> p c n", c=4)
    CGB = [carve(512 * 4, F32) for _ in range(2)]
    ZL = 516
    ZB = carve(4 * ZL * 4, F32, "p (c n) -> p c n", c=4)
    PT = [carve(UL * 4, F32) for _ in range(2)]
    CV = [carve(512 * 4, F32) for _ in range(2)]
    DT = carve(4 * 512 * 2, BF16, "p (c n) -> p c n", c=4)
    MIX = carve(8 * 512 * 2, BF16, "p (k n) -> p k n", k=8)
    endA = cur[0]
    W1B = [carve(8 * 1024 * 2, BF16, "p (k f) -> p k f", k=8, at=R0 + i * 16384) for i in range(2)]
    W2B = [carve(8 * 1024 * 2, BF16, "p (c d) -> p c d", c=8, at=R0 + 32768 + i * 16384) for i in range(2)]
    o = R0 + 65536
    A2T = [carve(8 * 512 * 2, BF16, "p (c n) -> p c n", c=8, at=o + i * 8192) for i in range(2)]
    o += 16384
    RB = [carve(512 * 4, F32, at=o + i * 2048) for i in range(2)]
    o += 4096
    FG = carve(D * 4, F32, at=o); o += 4096
    JUNK = carve(D * 2, BF16, at=o); o += 2048
    assert o <= TOTAL and endA <= TOTAL, (o, endA)
    W2B1_ALIAS = ("PW", "HN", "JA", "HNT", "UB")
    MIXERS = ("UB", "CGB", "ZB", "PT", "CV")
    assert (endA - R0) >= 0 and (R0 + 92160) <= (R0 + 93632)

    def h2t_f32(k, lo, n):
        return H2T[:, k, 512 + lo // 2:512 + lo // 2 + 2 * n].bitcast(F32)

    def h2t_bf(k, lo, n):
        return H2T[:, k, 512 + lo // 2:512 + lo // 2 + n]

    UBS0 = h2t_f32(0, 0, 2 * 304).rearrange("p (c n) -> p c n", c=2)
    UBS1 = h2t_f32(1, 0, 2 * 304).rearrange("p (c n) -> p c n", c=2)
    ZBS = h2t_f32(2, 0, 4 * 96).rearrange("p (c n) -> p c n", c=4)
    CVS = [h2t_f32(2, 1536 + i * 256, 64) for i in range(4)]
    PTS = [h2t_f32(3, i * 1216, 304) for i in range(2)]
    CGBS = [h2t_f32(4, i * 256, 64) for i in range(2)]
    DTS = h2t_bf(4, 512, 256).rearrange("p (c n) -> p c n", c=4)
    MIXS = h2t_bf(4, 1024, 512).rearrange("p (k n) -> p k n", k=8)
    HNTS = h2t_bf(4, 2048, 512).rearrange("p (k n) -> p k n", k=8)
    SL = [h2t_f32(5 + i, 0, 512) for i in range(3)]

    class UBsplit:
        def __getitem__(self, idx):
            p, c, n = idx
            if isinstance(c, slice):
                raise TypeError
            return (UBS0 if c < 2 else UBS1)[p, c % 2, n]

    BM = dict(UB=UB, ZB=ZB, CGB=CGB, CV=CV, PT=PT, DT=DT, MIX=MIX, HNT=HNT, n="")
    BS = dict(UB=UBsplit(), ZB=ZBS, CGB=CGBS, CV=CVS, PT=PTS, DT=DTS, MIX=MIXS, HNT=HNTS, n="S")
    S_PRIVATE = ("UBS", "ZBS", "CGBS", "CVS", "PTS", "DTS", "MIXS", "HNTS", "SL")

    psall = nc.alloc_psum_tensor("psall", [128, 4096], F32).ap()
    NFM = 4
    psFM = [psall[:, i * 512:(i + 1) * 512] for i in range(NFM)]
    psTM = [psall[:, 2048:3072], psall[:, 3072:4096]]
    psT = [psall[:, 3072:3584].bitcast(BF16), psall[:, 3584:4096].bitcast(BF16),
           psall[:, 2048:2560].bitcast(BF16), psall[:, 2560:3072].bitcast(BF16)]
    TMK = [[("ps", 4), ("ps", 5)], [("ps", 6), ("ps", 7)]]
    TK = [("ps", 6), ("ps", 7), ("ps", 4), ("ps", 5)]

    P = Prog()
    cnt = {"fm": 0, "tm": 0, "t": 0, "slot": 0, "hn": 0}

    def nxt(name, mod):
        v = cnt[name] % mod
        cnt[name] += 1
        return v

    groups = []
    groups.append(dict(name="meta", N=32, tiles=[(15, 32)], stride=1, H=16, Hz=2, col0=None, kind="meta"))
    groups.append(dict(name="P0", N=512, tiles=[(i, 128) for i in range(4)], stride=1, H=16, Hz=2, col0=0, kind="prompt", last=False))
    groups.append(dict(name="S", N=64, tiles=[(16, 64)], stride=16, H=240, Hz=32, col0=SEQ, kind="sample"))
    for g in range(1, 4):
        groups.append(dict(name="P%d" % g, N=512, tiles=[(4 * g + i, 128) for i in range(4)], stride=1, H=16, Hz=2,
                           col0=512 * g, kind="prompt", last=(g == 3)))

    for G in groups:
        G["B"] = BS if G["kind"] == "sample" else BM

    spf = sp.rearrange("s r f -> (s r) f")
    scf = sc.rearrange("s r f -> (s r) f")

    def load_state():
        P.dma("sync", lambda e: e.dma_start(out=SL[0][0:120, :], in_=spf[0:120, :]), writes=[("SL", 0)], sem="stateA")
        P.dma("sync", lambda e: e.dma_start(out=SL[1][0:120, :], in_=spf[120:240, :]), writes=[("SL", 1)], sem="stateB")
        P.dma("sync", lambda e: e.dma_start(out=SL[2][0:32, :], in_=scf), writes=[("SL", 2)], sem="stateC")

    CST = PT[1]
    P.dma("gpsimd", lambda e: e.dma_start(out=CST[0:8, 0:128], in_=g1.rearrange("(k p) -> k p", p=128)), writes=[("CSTsub", 0)], sem="cst")
    P.dma("gpsimd", lambda e: e.dma_start(out=CST[8:16, 0:128], in_=g2.rearrange("(k p) -> k p", p=128)), writes=[("CSTsub", 1)], sem="cst")
    P.dma("gpsimd", lambda e: e.dma_start(out=CST[16:20, 0:128], in_=pool_scale.rearrange("(g p) -> g p", p=128)), writes=[("CSTsub", 2)], sem="cst")
    P.dma("gpsimd", lambda e: e.dma_start(out=CST[20:32, 0:128], in_=conv_w.rearrange("j (c p) -> (j c) p", p=128)), writes=[("CSTsub", 3)], sem="cst")
    P.dma("sync", lambda e: e.dma_start(out=X1[0:16, 15, :], in_=meta), writes=[("X1sub", 15, 0)], sem="xmeta")
    P.dma("sync", lambda e: e.dma_start(out=X1[16:32, 15, :], in_=meta), writes=[("X1sub", 15, 1)], sem="xmeta")

    def load_xp(g, gate=()):
        P.dma("sync", lambda e, g=g: e.dma_start(out=X1[:, 4 * g:4 * g + 4, :],
                                                  in_=xp[512 * g:512 * (g + 1), :].rearrange("(t p) d -> p t d", p=128)),
              reads=list(gate), writes=[("X1", 4 * g + i) for i in range(4)], sem="xP%d" % g)

    load_xp(0)

    qorder = [(0, 0), (2, 1024), (3, 1536), (1, 512)]
    for qi, e0 in qorder:
        P.dma("gpsimd", lambda e, e0=e0: e.dma_start(out=WIN[:, :, e0:e0 + 512],
                                                      in_=w_in[:, e0:e0 + 512].rearrange("(k p) e -> p k e", p=128)),
              writes=[("WIN", qi)], sem="win%d" % qi)
    P.dma("gpsimd", lambda e: e.dma_start(out=PW, in_=pool_w.rearrange("g c d -> c g d")), writes=[("PW",)], sem="pw")
    P.dma("gpsimd", lambda e: e.dma_start(out=WOUT, in_=w_out.rearrange("(k p) e -> p k e", p=128)), writes=[("WOUT",)], sem="wout")

    P.add("gpsimd", lambda e: e.memset(identf, 0.0), writes=[("identf",)])
    P.add("gpsimd", lambda e: e.affine_select(out=identf, in_=identf, compare_op=ALU.not_equal, fill=1.0, base=0,
                                               pattern=[[-1, 128]], channel_multiplier=1),
          reads=[("identf",)], writes=[("identf",)])
    P.add("vector", lambda e: e.tensor_copy(out=identb, in_=identf), reads=[("identf",)], writes=[("identb",)])
    P.add("tensor", lambda e: e.transpose(out=psFM[0][:, 0:32], in_=CST[0:32, 0:128], identity=identf[0:32, 0:32]),
          reads=[("PT", 1), ("identf",)] + [("CSTsub", i) for i in range(4)], writes=[("ps", 0)])
    P.add("vector", lambda e: e.tensor_copy(out=CONST, in_=psFM[0][:, 0:32]), reads=[("ps", 0)], writes=[("CONST",)])
    cnt["fm"] = 1
    CK = ("CONST",)
    for c in range(4):
        P.add("vector", lambda e, c=c: e.tensor_scalar(out=PSC[:, c:c + 1], in0=psc[:, c:c + 1], scalar1=1.0 / POOL_WIN[c], scalar2=None,
                                                        op0=ALU.mult), reads=[CK], writes=[("PSC",)])

    def stats(src, pn, src_keys, junk, junk_key):
        slot = nxt("slot", NSLOT)
        P.add("scalar", lambda e: e.activation(out=junk[0:pn, :], in_=src, func=AF.Square, accum_out=ss[0:pn, slot:slot + 1]),
              reads=src_keys, writes=[junk_key, ("ss", slot)])
        P.add("scalar", lambda e: e.activation(out=sd[0:pn, slot:slot + 1], in_=ss[0:pn, slot:slot + 1], func=AF.Sqrt,
                                               bias=EPS, scale=1.0 / D),
              reads=[("ss", slot)], writes=[("sd", slot)])
        P.add("vector", lambda e: e.reciprocal(out=rs[0:pn, slot:slot + 1], in_=sd[0:pn, slot:slot + 1]),
              reads=[("sd", slot)], writes=[("rs", slot)])
        return slot

    def scale_to_hn(src, pn, src_keys, slot):
        hi = nxt("hn", 2)
        hn = HN[hi][0:pn, :]
        P.add("scalar", lambda e: e.activation(out=hn, in_=src, func=AF.Copy, scale=rs[0:pn, slot:slot + 1]),
              reads=src_keys + [("rs", slot)], writes=[("HN", hi)])
        return hn, ("HN", hi)

    def transpose_to_fm(hn, hn_key, pn, gT, dst, dst_keys):
        tb = nxt("t", 4)
        for k in range(8):
            P.add("tensor", lambda e, k=k: e.transpose(out=psT[tb][:, k * 128:k * 128 + pn], in_=hn[:, k * 128:(k + 1) * 128],
                                                       identity=identb[0:pn, 0:pn]),
                  reads=[hn_key, ("identb",)], writes=[TK[tb]])
        src = psT[tb].rearrange("p (k n) -> p k n", k=8)[:, :, 0:pn]
        P.add("vector", lambda e: e.tensor_tensor(out=dst, in0=src, in1=gT.unsqueeze(2).to_broadcast([128, 8, pn]), op=ALU.mult),
              reads=[TK[tb], CK], writes=dst_keys)

    def fm_matmul(lhs_fn, rhs_fn, nk, N, reads):
        b = nxt("fm", NFM)
        for k in range(nk):
            P.add("tensor", lambda e, k=k: e.matmul(out=psFM[b][:, 0:N], lhsT=lhs_fn(k), rhs=rhs_fn(k), start=(k == 0), stop=(k == nk - 1)),
                  reads=reads, writes=[("ps", b)])
        return b, psFM[b][:, 0:N]

    def xkeys(G, ti):
        if G["kind"] == "sample":
            return [("X1", ti)] + [("X1sub", 16, t) for t in range(TS)]
        if G["kind"] == "meta":
            return [("X1", ti), ("X1sub", 15, 0), ("X1sub", 15, 1)]
        return [("X1", ti)]

    def pre_stats1(G, tiles=None):
        G.setdefault("slots1", {})
        for i, (ti, pn) in enumerate(G["tiles"]):
            if (tiles is not None and i not in tiles) or i in G["slots1"]:
                continue
            G["slots1"][i] = stats(X1[0:pn, ti, :], pn, xkeys(G, ti), JA, ("JA",))

    HM = MIX[:, 7, 0:256].rearrange("p (k n) -> p k n", k=8)

    def hnt_key(G, i):
        return ("MIX", 7) if G["kind"] == "meta" else ("HNT" + G["B"]["n"], i)

    def T1_prescale(G, n=2):
        G.setdefault("hn1", {})
        for i, (ti, pn) in enumerate(G["tiles"][:n]):
            if i not in G["hn1"]:
                G["hn1"][i] = scale_to_hn(X1[0:pn, ti, :], pn, xkeys(G, ti), G["slots1"][i])

    def T1(G):
        B = G["B"]; sfx = B["n"]
        UB = B["UB"]; ZB = B["ZB"]; CGB = B["CGB"]; CV = B["CV"]; PT = B["PT"]; DT = B["DT"]; MIX = B["MIX"]; HNT = B["HNT"]
        KUB = "UB" + sfx; KZB = "ZB" + sfx; KCGB = "CGB" + sfx; KCV = "CV" + sfx; KPT = "PT" + sfx; KDT = "DT" + sfx
        KMIX = "MIX" + sfx; KHNT = "HNT" + sfx; NCV = len(CV)
        G.setdefault("hn1", {})
        for i, (ti, pn) in enumerate(G["tiles"]):
            if i not in G["hn1"]:
                G["hn1"][i] = scale_to_hn(X1[0:pn, ti, :], pn, xkeys(G, ti), G["slots1"][i])
            hn, hk = G["hn1"][i]
            dst = HM[:, :, 0:pn] if G["kind"] == "meta" else HNT[:, :, i * 128:i * 128 + pn]
            transpose_to_fm(hn, hk, pn, g1T, dst, [hnt_key(G, i)])
            if i + 2 < len(G["tiles"]):
                T1_prescale(G, i + 3)

    class T2Item:
        def __init__(self, x1, pn, ti, slot, c0):
            self.x1 = x1; self.pn = pn; self.ti = ti; self.slot = slot; self.c0 = c0; self.hn = None

        def scale(self):
            if self.hn is None:
                self.hn = scale_to_hn(self.x1, self.pn, [("X1", self.ti)], self.slot)

        def trans(self):
            self.scale()
            hn, hk = self.hn
            extra = P.keys_of(*S_PRIVATE) if 4 <= self.ti < 16 else []
            transpose_to_fm(hn, hk, self.pn, g2T, H2T[:, :, self.c0:self.c0 + self.pn], [("H2T", self.ti)] + extra)

    def pop_pending(pending, n=1):
        for _ in range(n):
            if pending:
                it = pending.pop(0)
                it.trans()
                if len(pending) >= 2:
                    pending[1].scale()
                elif len(pending) == 1:
                    pending[0].scale()

    def P2(G, pending, nxtG, deferred=None, stages=("in", "u", "cgh", "rest")):
        B = G["B"]; sfx = B["n"]
        UB = B["UB"]; ZB = B["ZB"]; CGB = B["CGB"]; CV = B["CV"]; PT = B["PT"]; DT = B["DT"]; MIX = B["MIX"]; HNT = B["HNT"]
        KUB = "UB" + sfx; KZB = "ZB" + sfx; KCGB = "CGB" + sfx; KCV = "CV" + sfx; KPT = "PT" + sfx; KDT = "DT" + sfx
        KMIX = "MIX" + sfx; KHNT = "HNT" + sfx; NCV = len(CV)
        N = G["N"]; s = G["stride"]; H = G["H"]; Hz = G["Hz"]; kind = G["kind"]
        zo = G.get("zo", 0)
        L = H + N
        mixing = kind != "meta"
        if kind == "sample" and "in" in stages:
            for c in range(4):
                for half in range(2):
                    b = nxt("fm", NFM)
                    P.add("tensor", lambda e, c=c, half=half, b=b: e.transpose(
                        out=psFM[b][:, 0:120], in_=SL[half][0:120, c * 128:(c + 1) * 128], identity=identf[0:120, 0:120]),
                        reads=[("SL", half), ("identf",)], writes=[("ps", b)])
                    dst = UB[:, c, 0:240].rearrange("p (r s) -> p s r", s=16)[:, half * 8:(half + 1) * 8, :]
                    P.add("scalar", lambda e, dst=dst, b=b: e.copy(out=dst, in_=psFM[b][:, 0:120].rearrange("p (s r) -> p s r", r=15)),
                          reads=[("ps", b)], writes=[(KUB, c, "halo")])
                b = nxt("fm", NFM)
                P.add("tensor", lambda e, c=c, b=b: e.transpose(out=psFM[b][:, 0:32], in_=SL[2][0:32, c * 128:(c + 1) * 128],
                                                                identity=identf[0:32, 0:32]),
                      reads=[("SL", 2), ("identf",)], writes=[("ps", b)])
                dstz = ZB[:, c, 0:32].rearrange("p (r s) -> p s r", s=16)
                P.add("scalar", lambda e, dstz=dstz, b=b: e.copy(out=dstz, in_=psFM[b][:, 0:32].rearrange("p (s r) -> p s r", r=2)),
                      reads=[("ps", b)], writes=[(KZB, c, "halo")])
        hnt_keys = [hnt_key(G, i) for i in range(len(G["tiles"]))]
        RH = HM if kind == "meta" else HNT

        def win_mm(e0, qi):
            return fm_matmul(lambda k: WIN[:, k, e0:e0 + 128], lambda k: RH[:, k, 0:N], 8, N, [("WIN", qi)] + hnt_keys)

        dstate = {}

        def pool_adds(c):
            w = POOL_WIN[c]
            u = UB[:, c, :]
            lo = H - (w - 2) * s
            cur_src, cur_keys = u, [(KUB, c, "halo"), (KUB, c, "new")]
            step = s
            bi = 0
            for lev in range({2: 1, 4: 2, 8: 3, 16: 4}[w]):
                dstb = PT[bi]
                P.add("gpsimd", lambda e, dstb=dstb, cur_src=cur_src, a0=lo, step=step: e.tensor_tensor(
                    out=dstb[:, a0:L], in0=cur_src[:, a0:L], in1=cur_src[:, a0 - step:L - step], op=ALU.add),
                    reads=cur_keys, writes=[(KPT, bi)])
                cur_src, cur_keys = dstb, [(KPT, bi)]
                lo += 2 * step
                step *= 2
                bi ^= 1
            tmp = PT[bi]
            P.add("gpsimd", lambda e: e.tensor_scalar(out=tmp[:, H:L], in0=u[:, H:L], scalar1=-float(w), scalar2=0.0, op0=ALU.mult, op1=ALU.add),
                  reads=[(KUB, c, "new")], writes=[(KPT, bi)])
            P.add("gpsimd", lambda e: e.tensor_tensor(out=DT[:, c, 0:N], in0=cur_src[:, H:L], in1=tmp[:, H:L], op=ALU.add),
                  reads=cur_keys + [(KPT, bi)], writes=[(KDT, c)])

        def halo_u():
            P.add("gpsimd", lambda e: e.tensor_copy(out=UB[:, :, 0:16], in_=UB[:, :, L - 16:L]),
                  reads=[(KUB, c, "new") for c in range(4)], writes=[(KUB, c, "halo") for c in range(4)])

        def halo_z():
            P.add("vector" if kind == "meta" else "gpsimd", lambda e: e.tensor_copy(out=ZB[:, :, 0:2], in_=ZB[:, :, Hz + N - 2:Hz + N]),
                  reads=[(KZB, c, "new") for c in range(4)], writes=[(KZB, c, "halo") for c in range(4)])

        if "u" in stages:
            for c in range(4):
                b, ps = win_mm(c * 128, 0)
                P.add("scalar", lambda e, c=c, ps=ps: e.copy(out=UB[:, c, H:L], in_=ps), reads=[("ps", b)], writes=[(KUB, c, "new")])
                if mixing:
                    pool_adds(c)
            pop_pending(pending)
            if kind == "meta":
                halo_u()

        def conv_chunk(c):
            z = ZB[:, c, :]
            zk = [(KZB, c, "halo"), (KZB, c, "new")]
            ci = c % NCV
            cv = CV[ci][:, 0:N]
            P.add("scalar", lambda e: e.activation(out=cv, in_=z[:, zo:zo + N], func=AF.Copy, scale=cw[:, 0, c:c + 1]),
                  reads=zk + [CK], writes=[(KCV, ci)])
            P.add("vector", lambda e: e.scalar_tensor_tensor(out=cv, in0=z[:, zo + s:zo + s + N], scalar=cw[:, 1, c:c + 1], in1=cv,
                                                             op0=ALU.mult, op1=ALU.add),
                  reads=zk + [CK, (KCV, ci)], writes=[(KCV, ci)])
            P.add("vector", lambda e: e.scalar_tensor_tensor(out=cv, in0=z[:, zo + 2 * s:zo + 2 * s + N], scalar=cw[:, 2, c:c + 1], in1=cv,
                                                             op0=ALU.mult, op1=ALU.add),
                  reads=zk + [CK, (KCV, ci)], writes=[(KCV, ci)])

        def cg_h(c):
            b, ps = win_mm(1024 + c * 128, 2)
            ci = c % 2
            P.add("scalar", lambda e, ps=ps: e.copy(out=CGB[ci][:, 0:N], in_=ps), reads=[("ps", b)], writes=[(KCGB, ci)])
            b2, ps2 = win_mm(1536 + c * 128, 3)
            P.add("vector", lambda e, ps2=ps2: e.tensor_tensor(out=ZB[:, c, Hz:Hz + N], in0=ps2, in1=CGB[ci][:, 0:N], op=ALU.mult),
                  reads=[("ps", b2), (KCGB, ci)], writes=[(KZB, c, "new")])

        def bg_chunk(c):
            ci = c % NCV
            cv = CV[ci][:, 0:N]
            b, ps = win_mm(512 + c * 128, 1)
            P.add("vector", lambda e, ps=ps: e.tensor_tensor(out=MIX[:, 4 + c, 0:N], in0=ps, in1=cv, op=ALU.mult),
                  reads=[("ps", b), (KCV, ci)], writes=[(KMIX, 4 + c)])

        if "cgh" in stages:
            nst = len(nxtG["tiles"]) if nxtG is not None else 0
            cg_h(0)
            if nst > 0:
                pre_stats1(nxtG, tiles=(0,))
            cg_h(1)
            if nst > 1:
                pre_stats1(nxtG, tiles=(1,))
            pop_pending(pending)
            early_conv = mixing and kind == "prompt" and "rest" in stages
            if early_conv:
                conv_chunk(0); conv_chunk(1)
            cg_h(2)
            if nst > 2:
                pre_stats1(nxtG, tiles=(2,))
            cg_h(3)
            if nst > 3:
                pre_stats1(nxtG, tiles=(3,))
            if deferred is not None:
                deferred()
            if "rest" in stages:
                pop_pending(pending)
            if kind == "meta":
                halo_z()
        else:
            early_conv = False
        if "rest" not in stages and "rest_a" not in stages and "rest_b" not in stages:
            return
        do_a = "rest" in stages or "rest_a" in stages
        do_b = "rest" in stages or "rest_b" in stages
        if mixing and do_a:
            if not early_conv:
                conv_chunk(0); conv_chunk(1)
            bg_chunk(0); bg_chunk(1)
            pop_pending(pending)
            if nxtG is not None and nxtG["kind"] == "prompt":
                T1_prescale(nxtG, 2 if not pending else 1)
            conv_chunk(2); conv_chunk(3)
            bg_chunk(2); bg_chunk(3)
        if do_a:
            pop_pending(pending, 8)
            if nxtG is not None and nxtG["kind"] == "prompt":
                T1_prescale(nxtG, 2)
        if not do_b:
            return
        if kind == "sample" or (kind == "prompt" and G.get("last")):
            if kind == "sample":
                ucols = (H, L); zcols = (Hz, Hz + N); nr = 64
            else:
                ucols = (L - 16, L); zcols = (Hz + N - 2, Hz + N); nr = 16
            nz = zcols[1] - zcols[0]
            stg = SL if kind == "sample" else CGB
            kstg = "SL" if kind == "sample" else KCGB
            b = nxt("fm", NFM)
            for c in range(4):
                P.add("tensor", lambda e, c=c, b=b: e.transpose(out=psFM[b][0:nr, c * 128:(c + 1) * 128], in_=UB[:, c, ucols[0]:ucols[1]],
                                                                identity=identf),
                      reads=[(KUB, c, "new"), ("identf",)], writes=[("ps", b)])
            P.add("scalar", lambda e, b=b: e.copy(out=stg[0][0:nr, :], in_=psFM[b][0:nr, :]), reads=[("ps", b)], writes=[(kstg, 0)])
            b2 = nxt("fm", NFM)
            for c in range(4):
                P.add("tensor", lambda e, c=c, b2=b2: e.transpose(out=psFM[b2][0:nz, c * 128:(c + 1) * 128], in_=ZB[:, c, zcols[0]:zcols[1]],
                                                                  identity=identf),
                      reads=[(KZB, c, "new"), ("identf",)], writes=[("ps", b2)])
            P.add("scalar", lambda e, b2=b2: e.copy(out=stg[1][0:nz, :], in_=psFM[b2][0:nz, :]), reads=[("ps", b2)], writes=[(kstg, 1)])
            if kind == "sample":
                for t in range(TS):
                    P.dma("sync", lambda e, t=t: e.dma_start(out=nps[:, 11 + t, :], in_=stg[0][t * 16:(t + 1) * 16, :]),
                          reads=[(kstg, 0)], sem="o_nps")
                for t in (2, 3):
                    P.dma("sync", lambda e, t=t: e.dma_start(out=ncs[:, t - 2, :], in_=stg[1][t * 16:(t + 1) * 16, :]),
                          reads=[(kstg, 1)], sem="o_ncs")
            else:
                P.dma("sync", lambda e: e.dma_start(out=npp, in_=stg[0][1:16, :]), reads=[(kstg, 0)], sem="o_npp")
                P.dma("sync", lambda e: e.dma_start(out=ncp, in_=stg[1][0:2, :]), reads=[(kstg, 1)], sem="o_ncp")
        if kind == "prompt" and not G.get("last"):
            halo_u()
            halo_z()

    def P3(G):
        B = G["B"]; sfx = B["n"]
        UB = B["UB"]; ZB = B["ZB"]; CGB = B["CGB"]; CV = B["CV"]; PT = B["PT"]; DT = B["DT"]; MIX = B["MIX"]; HNT = B["HNT"]
        KUB = "UB" + sfx; KZB = "ZB" + sfx; KCGB = "CGB" + sfx; KCV = "CV" + sfx; KPT = "PT" + sfx; KDT = "DT" + sfx
        KMIX = "MIX" + sfx; KHNT = "HNT" + sfx; NCV = len(CV)
        N = G["N"]
        for c in range(4):
            b, ps = fm_matmul(lambda k, c=c: PW[:, c, :], lambda k, c=c: DT[:, c, 0:N], 1, N, [("PW",), (KDT, c)])
            P.add("scalar", lambda e, c=c, ps=ps: e.activation(out=MIX[:, c, 0:N], in_=ps, func=AF.Copy, scale=PSC[:, c:c + 1]),
                  reads=[("ps", b), ("PSC",)], writes=[(KMIX, c)])

    def P4(G, pending):
        B = G["B"]; sfx = B["n"]
        UB = B["UB"]; ZB = B["ZB"]; CGB = B["CGB"]; CV = B["CV"]; PT = B["PT"]; DT = B["DT"]; MIX = B["MIX"]; HNT = B["HNT"]
        KUB = "UB" + sfx; KZB = "ZB" + sfx; KCGB = "CGB" + sfx; KCV = "CV" + sfx; KPT = "PT" + sfx; KDT = "DT" + sfx
        KMIX = "MIX" + sfx; KHNT = "HNT" + sfx; NCV = len(CV)
        for i, (ti, pn) in enumerate(G["tiles"]):
            tb = nxt("tm", 2)
            korder = (4, 5, 6, 7, 0, 1, 2, 3)
            for half in range(2):
                for kk, k in enumerate(korder):
                    P.add("tensor", lambda e, k=k, kk=kk, half=half, i=i, pn=pn, tb=tb: e.matmul(
                        out=psTM[tb][0:pn, half * 512:(half + 1) * 512], lhsT=MIX[:, k, i * 128:i * 128 + pn],
                        rhs=WOUT[:, k, half * 512:(half + 1) * 512], start=(kk == 0), stop=(kk == 7)),
                        reads=[(KMIX, k), ("WOUT",)], writes=TMK[tb])
            x1 = X1[0:pn, ti, :]
            P.add("vector", lambda e, x1=x1, pn=pn, tb=tb: e.tensor_tensor(out=x1, in0=psTM[tb][0:pn, :], in1=x1, op=ALU.add),
                  reads=TMK[tb] + [("X1", ti)], writes=[("X1", ti)])
            slot = stats(x1, pn, [("X1", ti)], JA, ("JA",))
            c0 = G["col0"] + i * 128

            pending.append(T2Item(x1, pn, ti, slot, c0))
            if len(pending) <= 2:
                pending[-1].scale()
        for it in pending[:2]:
            it.scale()

    pending = []
    deferred = [None]
    Gm, G0, GS, G1 = groups[0], groups[1], groups[2], groups[3]
    load_state()
    for t in range(TS):
        P.dma("sync", lambda e, t=t: e.dma_start(out=X1[t * 16:(t + 1) * 16, 16, :], in_=xs[:, t, :]),
              writes=[("X1sub", 16, t)], sem="xS")
    pre_stats1(Gm)
    pre_stats1(G0)
    T1(Gm)
    T1(G0)
    pre_stats1(GS)
    T1_prescale(GS, 1)
    P2(Gm, [], None, stages=("u",))
    P2(G0, pending, G1, stages=("u",))
    T1(GS)
    P2(GS, [], None, stages=("in", "u"))
    load_xp(1, gate=[("HNT", 3)])
    P2(Gm, [], None, stages=("cgh",))
    P2(G0, pending, G1, stages=("cgh",))
    P2(GS, [], None, stages=("cgh",))
    P2(GS, [], None, stages=("rest_a",))
    P2(G0, pending, G1, stages=("rest",))
    deferred[0] = (lambda: P4(GS, pending))
    load_xp(2, gate=[("MIX", 7)])
    P.dma("sync", lambda e: e.dma_start(out=nps[:, 0:11, :], in_=sp[:, 4:15, :]), sem="out")
    T1(G1)
    P3(G0)
    P3(GS)
    P2(GS, [], None, stages=("rest_b",))
    P4(G0, pending)
    for gi in range(3, len(groups)):
        G = groups[gi]
        nxtG = groups[gi + 1] if gi + 1 < len(groups) else None
        dfr = None
        if deferred[0] is not None:
            dfr, deferred[0] = deferred[0], None
        P2(G, pending, nxtG, dfr)
        if gi == 3:
            load_xp(3, gate=[("MIX", 7)])
        if nxtG is not None:
            T1(nxtG)
        P3(G)
        P4(G, pending)

    bgroups = [g for g in groups if g["kind"] == "sample"] + [g for g in groups if g["kind"] == "prompt"]
    NBLK = 4

    def load_block(b):
        i = b % 2
        extra1 = P.keys_of("WIN") if b < 2 else []
        if b == 0:
            extra2 = P.keys_of("WOUT")
        elif b == 1:
            extra2 = P.keys_of(*W2B1_ALIAS)
        else:
            extra2 = []
        P.dma("gpsimd", lambda e: e.dma_start(out=W1B[i], in_=w1[:, b * 1024:(b + 1) * 1024].rearrange("(k p) f -> p k f", p=128)),
              writes=[("W1B", i)] + extra1, sem="w1b%d" % i)
        P.dma("gpsimd", lambda e: e.dma_start(out=W2B[i], in_=w2[b * 1024:(b + 1) * 1024, :].rearrange("(c p) d -> p c d", p=128)),
              writes=[("W2B", i)] + extra2, sem="w2b%d" % i)

    order0 = [bgroups[1], bgroups[0]] + bgroups[2:]
    steps = [(b, G) for b in range(NBLK) for G in (order0 if b == 0 else bgroups)]
    first_b_write = [True]

    def w1_step(si, pending=None):
        b, G = steps[si]
        N = G["N"]; col0 = G["col0"]; ai = si % 2
        tkeys = [("H2T", ti) for ti, _ in G["tiles"]]
        for fc in range(8):
            bk, ps = fm_matmul(lambda k, fc=fc: W1B[b % 2][:, k, fc * 128:(fc + 1) * 128], lambda k: H2T[:, k, col0:col0 + N], 8, N,
                               [("W1B", b % 2)] + tkeys)
            ri = fc % 2
            extra = P.keys_of(*MIXERS) if first_b_write[0] else []
            first_b_write[0] = False
            P.add("scalar", lambda e, ri=ri, ps=ps: e.activation(out=RB[ri][:, 0:N], in_=ps, func=AF.Relu),
                  reads=[("ps", bk)], writes=[("RB", ri)] + extra)
            P.add("vector", lambda e, ri=ri, fc=fc: e.tensor_tensor(out=A2T[ai][:, fc, 0:N], in0=RB[ri][:, 0:N], in1=RB[ri][:, 0:N], op=ALU.mult),
                  reads=[("RB", ri)], writes=[("A2T", ai, fc)] + extra)
            if pending and fc % 2 == 1:
                pop_pending(pending)

    fins = []

    def w2_step(si):
        b, G = steps[si]
        ai = si % 2
        last = (b == NBLK - 1)
        for i, (ti, pn) in enumerate(G["tiles"]):
            tb = nxt("tm", 2)
            for half in range(2):
                for fc in range(8):
                    P.add("tensor", lambda e, fc=fc, half=half, i=i, pn=pn, tb=tb: e.matmul(
                        out=psTM[tb][0:pn, half * 512:(half + 1) * 512], lhsT=A2T[ai][:, fc, i * 128:i * 128 + pn],
                        rhs=W2B[b % 2][:, fc, half * 512:(half + 1) * 512], start=(fc == 0), stop=(fc == 7)),
                        reads=[("A2T", ai, fc), ("W2B", b % 2)], writes=TMK[tb])
            x1 = X1[0:pn, ti, :]
            P.add("vector", lambda e, x1=x1, pn=pn, tb=tb: e.tensor_tensor(out=x1, in0=psTM[tb][0:pn, :], in1=x1, op=ALU.add),
                  reads=TMK[tb] + [("X1", ti)], writes=[("X1", ti)])
            if last:
                slot = nxt("slot", NSLOT)
                P.add("scalar", lambda e, x1=x1, pn=pn, slot=slot: e.activation(out=JUNK[0:pn, :], in_=x1, func=AF.Square,
                                                                                accum_out=ss[0:pn, slot:slot + 1]),
                      reads=[("X1", ti)], writes=[("JUNK",), ("ss", slot)])
                P.add("scalar", lambda e, pn=pn, slot=slot: e.activation(out=sd[0:pn, slot:slot + 1], in_=ss[0:pn, slot:slot + 1],
                                                                         func=AF.Sqrt, bias=EPS, scale=1.0 / D),
                      reads=[("ss", slot)], writes=[("sd", slot)])

                def fin(x1=x1, pn=pn, slot=slot, ti=ti, G=G):
                    P.add("vector", lambda e: e.reciprocal(out=rs[0:pn, slot:slot + 1], in_=sd[0:pn, slot:slot + 1]),
                          reads=[("sd", slot)], writes=[("rs", slot)])
                    P.add("vector", lambda e: e.scalar_tensor_tensor(
                        out=x1, in0=x1, scalar=rs[0:pn, slot:slot + 1], in1=FG[0:pn, :], op0=ALU.mult, op1=ALU.mult),
                        reads=[("X1", ti), ("rs", slot), ("FG",)], writes=[("X1", ti)])
                    if G["kind"] == "prompt":
                        P.dma("sync", lambda e: e.dma_start(out=yp[ti * 128:(ti + 1) * 128, :], in_=X1[:, ti, :]),
                              reads=[("X1", ti)], sem="out")
                    else:
                        for t in range(TS):
                            P.dma("sync", lambda e, t=t: e.dma_start(out=ys[:, t, :], in_=X1[t * 16:(t + 1) * 16, ti, :]),
                                  reads=[("X1", ti)], sem="out")
                fins.append(fin)
                if len(fins) > 1:
                    fins.pop(0)()
        if si == ns - 1:
            while fins:
                fins.pop(0)()

    load_block(0)
    ns = len(steps)
    w1_step(0, pending)
    pop_pending(pending, 8)
    load_block(1)
    P.dma("sync", lambda e: e.dma_start(out=FG, in_=fg.partition_broadcast(128)), writes=[("FG",)] + P.keys_of(*MIXERS), sem="fgl")
    for si in range(ns):
        if si + 1 < ns:
            w1_step(si + 1)
        w2_step(si)
        b, G = steps[si]
        if G is bgroups[-1] and b + 2 < NBLK:
            load_block(b + 2)

    P.emit(nc, final_waits=["out", "o_nps", "o_ncs", "o_npp", "o_ncp"])
    return nc


_NC_CACHE = {}


def _get_nc():
    if "nc" not in _NC_CACHE:
        _NC_CACHE["nc"] = build_nc()
    return _NC_CACHE["nc"]


def kernel(x_prompt, x_sample, state_pool, state_conv, meta_tokens, norm1_g, w_in, pool_w, pool_scale, conv_w,
           w_out, norm2_g, w1, w2, final_g):
    f = lambda a: np.ascontiguousarray(np.asarray(a, dtype=np.float32))
    x_prompt = f(x_prompt); x_sample = f(x_sample); state_pool = f(state_pool); state_conv = f(state_conv)
    shared = {
        "meta": f(meta_tokens), "g1": f(norm1_g).reshape(D), "w_in": f(w_in).reshape(D, 2048),
        "pool_w": f(pool_w).reshape(4, 128, 128), "pool_scale": f(pool_scale).reshape(512),
        "conv_w": f(conv_w).reshape(3, 512), "w_out": f(w_out).reshape(D, D), "g2": f(norm2_g).reshape(D),
        "w1": f(w1).reshape(D, 4096), "w2": f(w2).reshape(4096, D), "fg": f(final_g).reshape(D),
    }
    in_maps = []
    for c in range(N_CORES):
        m = dict(shared)
        m["xp"] = x_prompt[c]
        m["xs"] = x_sample[c * NSEQ_S:(c + 1) * NSEQ_S]
        m["sp"] = state_pool[0, c * NSEQ_S:(c + 1) * NSEQ_S]
        m["sc"] = state_conv[0, c * NSEQ_S:(c + 1) * NSEQ_S]
        in_maps.append(m)
    nc = build_nc()
    res = run_bass_kernel_spmd(nc, in_maps, core_ids=list(range(N_CORES)))
    R = res.results
    y_prompt = np.stack([R[c]["yp"] for c in range(N_CORES)], axis=0)
    y_sample = np.concatenate([R[c]["ys"] for c in range(N_CORES)], axis=0)
    npp = np.stack([R[c]["npp"] for c in range(N_CORES)], axis=0)[None]
    ncp = np.stack([R[c]["ncp"] for c in range(N_CORES)], axis=0)[None]
    nps = np.concatenate([R[c]["nps"] for c in range(N_CORES)], axis=0)[None]
    ncs = np.concatenate([R[c]["ncs"] for c in range(N_CORES)], axis=0)[None]
    return (y_prompt.astype(np.float32), y_sample.astype(np.float32), npp.astype(np.float32), ncp.astype(np.float32),
            nps.astype(np.float32), ncs.astype(np.float32))
```
